# Optimizing a Trainium2 kernel written in Bass

```python
import functools
import jax
import jax.numpy as jnp
from jax import lax
import numpy as np

D_MODEL = 1024
BATCH = 2
SEQ = 8192
DEPTH = 2
DEC_BATCH = 32
DEC_SEQ = 4
PAST_LEN = 16384
PAGE_SIZE = 128

HEAD_DIM = 64
HEADS_PER_GROUP = 8
GROUPS = ((128, 1), (512, 4), (2048, 16))
N_GROUPS = 3
SPAN = 128
N_KEYS = SPAN + 1
ATTN_DIM = N_GROUPS * HEADS_PER_GROUP * HEAD_DIM
ATTN_OUT_DIM = HEADS_PER_GROUP * HEAD_DIM
CONV_DIM = D_MODEL
CONV_WIDTH = 3
D_FF = 2816
QBLOCK = 128
IN_DIM = 3 * ATTN_DIM + 3 * CONV_DIM + 2 * D_MODEL
RMS_EPS = 1e-6
FFN_RESIDUAL_WEIGHT = 0.5
ATTN_SCALE = HEAD_DIM ** -0.5
NEG = -1e30

kernel_name = "dilated_swa_shortconv_gated_hybrid_step"


def rmsnorm(x, g):
    xf = x.astype(jnp.float32)
    inv = lax.rsqrt(jnp.mean(xf * xf, axis=-1, keepdims=True) + RMS_EPS)
    return (xf * inv).astype(x.dtype) * g


def half_swiglu(x, g, w_in, w_out):
    h = rmsnorm(x, g) @ w_in
    gate, up = jnp.split(h, 2, axis=-1)
    return FFN_RESIDUAL_WEIGHT * ((jax.nn.silu(gate) * up) @ w_out)


def split_proj(u, w_in, b_gate):
    z = u @ w_in
    cuts = [ATTN_DIM, 2 * ATTN_DIM, 3 * ATTN_DIM, 3 * ATTN_DIM + CONV_DIM,
            3 * ATTN_DIM + 2 * CONV_DIM, 3 * ATTN_DIM + 3 * CONV_DIM]
    q, k, v, c_b, c_c, c_h, gates = jnp.split(z, cuts, axis=-1)
    heads = lambda a: a.reshape(a.shape[0], a.shape[1], N_GROUPS, HEADS_PER_GROUP, HEAD_DIM)
    g_attn, g_conv = jnp.split(jax.nn.sigmoid(gates + b_gate), 2, axis=-1)
    return heads(q), heads(k), heads(v), c_b, c_c, c_h, g_attn, g_conv


def dilated_attn_prompt(q, k, v, dil):
    b, s, h, e = q.shape
    L = s // dil
    nb = -(-L // QBLOCK)
    lp = nb * QBLOCK

    def by_stride(a):
        a = a.reshape(b, L, dil, h, e).transpose(0, 2, 1, 3, 4)
        a = jnp.pad(a, ((0, 0), (0, 0), (0, lp - L), (0, 0), (0, 0)))
        return a.reshape(b, dil, nb, QBLOCK, h, e)

    def with_prev(a):
        prev = jnp.pad(a[:, :, :-1], ((0, 0), (0, 0), (1, 0), (0, 0), (0, 0), (0, 0)))
        return jnp.concatenate([prev, a], axis=3)

    qs = by_stride(q)
    kb = with_prev(by_stride(k))
    vb = with_prev(by_stride(v))
    scores = jnp.einsum('brnqhe,brnkhe->brnhqk', qs, kb,
                        preferred_element_type=jnp.float32) * ATTN_SCALE
    qi = jnp.arange(QBLOCK)[:, None]
    ki = jnp.arange(2 * QBLOCK)[None, :]
    rel = QBLOCK + qi - ki
    key_u = jnp.arange(nb)[:, None, None] * QBLOCK - QBLOCK + ki[None]
    valid = (rel >= 0) & (rel <= SPAN) & (key_u >= 0)
    scores = jnp.where(valid[:, None], scores, NEG)
    lse = jax.nn.logsumexp(scores, axis=-1)
    p = jnp.exp(scores - lse[..., None])
    o = jnp.einsum('brnhqk,brnkhe->brnqhe', p, vb.astype(jnp.float32))
    o = o.reshape(b, dil, lp, h, e)[:, :, :L].transpose(0, 2, 1, 3, 4).reshape(b, s, h, e)
    lse = lse.transpose(0, 1, 2, 4, 3).reshape(b, dil, lp, h)[:, :, :L]
    lse = lse.transpose(0, 2, 1, 3).reshape(b, s, h)
    return o, lse


def dilated_attn_sample(q, k_new, v_new, kv_buf, dil, window):
    t = q.shape[1]
    lb = kv_buf.shape[1]
    kc = jnp.concatenate([kv_buf[:, :, 0], k_new], axis=1)
    vc = jnp.concatenate([kv_buf[:, :, 1], v_new], axis=1)
    idx = lb + jnp.arange(t)[:, None] - dil * jnp.arange(N_KEYS)[None, :]
    valid = idx >= 0
    idx = jnp.maximum(idx, 0)
    kg = kc[:, idx]
    vg = vc[:, idx]
    scores = jnp.einsum('bthe,btjhe->bthj', q, kg,
                        preferred_element_type=jnp.float32) * ATTN_SCALE
    scores = jnp.where(valid[:, None, :], scores, NEG)
    lse = jax.nn.logsumexp(scores, axis=-1)
    p = jnp.exp(scores - lse[..., None])
    o = jnp.einsum('bthj,btjhe->bthe', p, vg.astype(jnp.float32))
    keep = min(window, lb + t)
    new_buf = jnp.stack([kc[:, -keep:], vc[:, -keep:]], axis=2)
    return o, lse, new_buf


def merge_groups(outs, lses, dtype):
    o = jnp.stack(outs, axis=2)
    w = jax.nn.softmax(jnp.stack(lses, axis=2), axis=2)
    o = jnp.sum(w[..., None] * o, axis=2)
    return o.reshape(o.shape[0], o.shape[1], ATTN_OUT_DIM).astype(dtype)


def attend_prompt(q, k, v):
    outs, lses, bufs = [], [], []
    s = k.shape[1]
    for g, (win, dil) in enumerate(GROUPS):
        o, l = dilated_attn_prompt(q[:, :, g], k[:, :, g], v[:, :, g], dil)
        outs.append(o)
        lses.append(l)
        keep = min(win, s)
        bufs.append(jnp.stack([k[:, -keep:, g], v[:, -keep:, g]], axis=2))
    return merge_groups(outs, lses, q.dtype), bufs


def attend_sample(q, k, v, caches):
    outs, lses, bufs = [], [], []
    for g, (win, dil) in enumerate(GROUPS):
        o, l, nb_ = dilated_attn_sample(q[:, :, g], k[:, :, g], v[:, :, g], caches[g], dil, win)
        outs.append(o)
        lses.append(l)
        bufs.append(nb_)
    return merge_groups(outs, lses, q.dtype), bufs


def short_conv(zc, w):
    t = zc.shape[1] - (CONV_WIDTH - 1)
    return w[0] * zc[:, 0:t] + w[1] * zc[:, 1:t + 1] + w[2] * zc[:, 2:t + 2]


def block(x, attend, conv_prev, w):
    (n1, f1i, f1o, nm, w_in, b_gate, conv_w, w_ao, w_co, w_o, n2, f2i, f2o) = w
    x = x + half_swiglu(x, n1, f1i, f1o)
    u = rmsnorm(x, nm)
    q, k, v, c_b, c_c, c_h, g_attn, g_conv = split_proj(u, w_in, b_gate)
    a, kv_bufs = attend(q, k, v)
    zc = jnp.concatenate([conv_prev.astype(x.dtype), c_c * c_h], axis=1)
    c = c_b * short_conv(zc, conv_w)
    new_conv = zc[:, -(CONV_WIDTH - 1):]
    m = g_attn * (a @ w_ao) + g_conv * (c @ w_co)
    x = x + m @ w_o
    x = x + half_swiglu(x, n2, f2i, f2o)
    return x, kv_bufs, new_conv


def setup_inputs(seed: int = 0) -> dict:
    key = jax.random.key(seed)
    ks = jax.random.split(key, 22)
    f32 = jnp.float32
    nrm = lambda kk, shape, scale: jax.random.normal(kk, shape, f32) * scale
    lens = [min(win, PAST_LEN) for win, _ in GROUPS]
    kvshape = lambda n: (DEPTH, DEC_BATCH, n, 2, HEADS_PER_GROUP, HEAD_DIM)
    return {
        'x_prompt': nrm(ks[0], (BATCH, SEQ, D_MODEL), 1.0),
        'x_sample': nrm(ks[1], (DEC_BATCH, DEC_SEQ, D_MODEL), 1.0),
        'cache_kv1': nrm(ks[2], kvshape(lens[0]), 1.0),
        'cache_kv2': nrm(ks[3], kvshape(lens[1]), 1.0),
        'cache_kv3': nrm(ks[4], kvshape(lens[2]), 1.0),
        'state_conv': nrm(ks[5], (DEPTH, DEC_BATCH, CONV_WIDTH - 1, CONV_DIM), 1.0),
        'ffn1_norm': 1.0 + nrm(ks[6], (DEPTH, D_MODEL), 0.01),
        'ffn1_w_in': nrm(ks[7], (DEPTH, D_MODEL, 2 * D_FF), D_MODEL ** -0.5),
        'ffn1_w_out': nrm(ks[8], (DEPTH, D_FF, D_MODEL), D_FF ** -0.5),
        'mix_norm': 1.0 + nrm(ks[9], (DEPTH, D_MODEL), 0.01),
        'w_in': nrm(ks[10], (DEPTH, D_MODEL, IN_DIM), D_MODEL ** -0.5),
        'b_gate': nrm(ks[11], (DEPTH, 2 * D_MODEL), 0.1),
        'conv_w': nrm(ks[12], (DEPTH, CONV_WIDTH, CONV_DIM), CONV_WIDTH ** -0.5),
        'w_attn_out': nrm(ks[13], (DEPTH, ATTN_OUT_DIM, D_MODEL), ATTN_OUT_DIM ** -0.5),
        'w_conv_out': nrm(ks[14], (DEPTH, CONV_DIM, D_MODEL), CONV_DIM ** -0.5),
        'w_out': nrm(ks[15], (DEPTH, D_MODEL, D_MODEL), D_MODEL ** -0.5),
        'ffn2_norm': 1.0 + nrm(ks[16], (DEPTH, D_MODEL), 0.01),
        'ffn2_w_in': nrm(ks[17], (DEPTH, D_MODEL, 2 * D_FF), D_MODEL ** -0.5),
        'ffn2_w_out': nrm(ks[18], (DEPTH, D_FF, D_MODEL), D_FF ** -0.5),
        'final_norm': 1.0 + nrm(ks[19], (D_MODEL,), 0.01),
    }


def reference(x_prompt, x_sample, cache_kv1, cache_kv2, cache_kv3, state_conv,
              ffn1_norm, ffn1_w_in, ffn1_w_out, mix_norm, w_in, b_gate, conv_w,
              w_attn_out, w_conv_out, w_out, ffn2_norm, ffn2_w_in, ffn2_w_out, final_norm):
    yp, ys = x_prompt, x_sample
    kvp = [[], [], []]
    kvs = [[], [], []]
    convp, convs = [], []
    for l in range(DEPTH):
        w = (ffn1_norm[l], ffn1_w_in[l], ffn1_w_out[l], mix_norm[l], w_in[l], b_gate[l],
             conv_w[l], w_attn_out[l], w_conv_out[l], w_out[l],
             ffn2_norm[l], ffn2_w_in[l], ffn2_w_out[l])
        zeros_ctx = jnp.zeros((yp.shape[0], CONV_WIDTH - 1, CONV_DIM), yp.dtype)
        yp, bufs_p, cv_p = block(yp, attend_prompt, zeros_ctx, w)
        att_s = functools.partial(attend_sample, caches=(cache_kv1[l], cache_kv2[l], cache_kv3[l]))
        ys, bufs_s, cv_s = block(ys, att_s, state_conv[l], w)
        for g in range(N_GROUPS):
            kvp[g].append(bufs_p[g])
            kvs[g].append(bufs_s[g])
        convp.append(cv_p)
        convs.append(cv_s)
    yp = rmsnorm(yp, final_norm)
    ys = rmsnorm(ys, final_norm)
    return (yp, ys,
            jnp.stack(kvp[0]), jnp.stack(kvp[1]), jnp.stack(kvp[2]), jnp.stack(convp),
            jnp.stack(kvs[0]), jnp.stack(kvs[1]), jnp.stack(kvs[2]), jnp.stack(convs))
```

```python
import numpy as np
import ml_dtypes
import concourse.bass as bass
import concourse.mybir as mybir
from concourse.bass_utils import run_bass_kernel_spmd

F32, BF16 = mybir.dt.float32, mybir.dt.bfloat16
AF = mybir.ActivationFunctionType
ALU = mybir.AluOpType
AX = mybir.AxisListType

NTOK, NP, NS = 2064, 2048, 16
PG = 2080
NPAGE = 12
TILES = [(0, 512), (512, 512), (1024, 512), (1536, 512), (2048, 16)]
DIL = [1, 4, 16]
LB = [128, 512, 2048]
HK = [128, 512, 2048]
NHB = [1, 4, 16]
KOFF = {}
_o = 0
for _g in (2, 1, 0):
    for _pr in range(4):
        KOFF[(_g, _pr)] = _o
        _o += HK[_g]
VOFF = {}
for _g in range(3):
    VOFF[_g] = _o
    _o += NHB[_g] * 512
ZOFF = _o
PW = _o + 16
CH = 4096
NCH = (PW + CH - 1) // CH
EPS = 1e-6
SAME_ENG_SYNC = True
import os
STAGE = int(os.environ.get("KSTAGE", "99"))


class Buf:
    __slots__ = ("name", "w", "rc", "rd", "excl")

    def __init__(self, name, excl=False):
        self.name = name
        self.excl = excl
        self.w = None
        self.rc = {}
        self.rd = []


class Op:
    __slots__ = ("eng", "fn", "deps", "signal", "seq", "dma", "dsem", "dval", "idx")


class Sched:
    ENGS = ("pe", "act", "dve", "pool", "sp")
    NDS = {"sp": 8, "pool": 8, "act": 2}
    SEMCAP = 12000

    def __init__(self):
        self.q = {e: [] for e in self.ENGS}

    def add(self, eng, fn, reads=(), writes=(), dma=False, append=False):
        op = Op()
        op.eng, op.fn, op.dma, op.signal, op.seq = eng, fn, dma, False, None
        writes = list(writes) + [b for b in reads if b.excl]
        reads = [b for b in reads if not b.excl]
        deps = []
        for b in reads:
            if b.w is not None:
                deps.extend(b.w)
        for b in writes:
            if b.w is not None and not append:
                deps.extend(b.w)
            deps.extend(b.rc.values())
            deps.extend(b.rd)
        dd = {}
        dl = []
        for d in deps:
            if d.dma:
                dl.append(d)
            else:
                o = dd.get(d.eng)
                if o is None or d.idx > o.idx:
                    dd[d.eng] = d
        op.deps = list(dd.values()) + list({id(d): d for d in dl}.values())
        for d in op.deps:
            d.signal = True
        op.idx = len(self.q[eng])
        for b in reads:
            if dma:
                b.rd.append(op)
            else:
                b.rc[eng] = op
        for b in writes:
            if append and b.w is not None:
                b.w = b.w + [op]
            else:
                b.w = [op]
            b.rc = {}
            b.rd = []
        self.q[eng].append(op)
        return op

    def emit(self, nc, stack):
        ncs = {}
        for e in self.ENGS:
            n = sum(1 for o in self.q[e] if (not o.dma) and o.signal)
            ncs[e] = [stack.enter_context(nc.semaphore(f"c_{e}_{i}")) for i in range(n // self.SEMCAP + 1)]
        dsem = {e: [stack.enter_context(nc.semaphore(f"d_{e}_{i}")) for i in range(k)] for e, k in self.NDS.items()}
        for e in self.ENGS:
            cnt = 0
            dc = 0
            dcount = {}
            for o in self.q[e]:
                if o.dma:
                    k = dc % self.NDS[e]
                    dc += 1
                    dcount[k] = dcount.get(k, 0) + 1
                    o.dsem = (e, k)
                    o.dval = 16 * dcount[k]
                elif o.signal:
                    o.seq = (cnt // self.SEMCAP, cnt % self.SEMCAP + 1)
                    cnt += 1
        block = stack.enter_context(nc.Block())
        q = self.q

        def run(ename, e):
            known = {}
            for o in q[ename]:
                need = {}
                for d in o.deps:
                    if d.dma:
                        key, val = ("d",) + d.dsem, d.dval
                    else:
                        if d.eng == ename and (ename == "pe" or not SAME_ENG_SYNC):
                            continue
                        key, val = ("c", d.eng), d.seq
                    if key in known and known[key] >= val:
                        continue
                    if key not in need or need[key] < val:
                        need[key] = val
                if o.dma and o.dval > 16:
                    key, val = ("d",) + o.dsem, o.dval - 16
                    if not (key in known and known[key] >= val):
                        if key not in need or need[key] < val:
                            need[key] = val
                for key, val in need.items():
                    if key[0] == "d":
                        e.wait_ge(dsem[key[1]][key[2]], val)
                    else:
                        e.wait_ge(ncs[key[1]][val[0]], val[1])
                    known[key] = val
                ins = o.fn(e)
                if o.dma:
                    ins.then_inc(dsem[o.dsem[0]][o.dsem[1]], 16)
                elif o.signal:
                    ins.then_inc(ncs[ename][o.seq[0]], 1)
            if ename in self.NDS:
                last = {}
                for o in q[ename]:
                    if o.dma:
                        last[o.dsem] = o.dval
                for (en, k), v in last.items():
                    e.wait_ge(dsem[en][k], v)

        @block.tensor
        def _(e):
            run("pe", e)

        @block.scalar
        def _(e):
            run("act", e)

        @block.vector
        def _(e):
            run("dve", e)

        @block.gpsimd
        def _(e):
            run("pool", e)

        @block.sync
        def _(e):
            run("sp", e)


def build():
    from contextlib import ExitStack
    nc = bass.Bass("TRN2", target_bir_lowering=False)
    S = Sched()
    stack = ExitStack()

    MINI = bool(os.environ.get("KMINI"))

    def din(name, shape, dt=F32):
        if MINI and name not in ("xin", "cst_f", "cst_b"):
            shape = [2, 128, 128] if len(shape) == 3 else [2, 4, 4, 128]
        return nc.dram_tensor(name, list(shape), dt, kind="ExternalInput").ap()

    def dout(name, shape, dt=F32):
        if MINI and name != "y":
            shape = [2, 8, 128]
        return nc.dram_tensor(name, list(shape), dt, kind="ExternalOutput").ap()

    def dint(name, shape, dt=BF16):
        if os.environ.get("KMINI2"):
            shape = [4, 128, 128] if len(shape) == 3 else [128, 128]
        return nc.dram_tensor(name, list(shape), dt)

    xin = din("xin", [NTOK, 1024])
    ck = [din(f"ck{g}", [2, 4, LB[g], 1024]) for g in range(3)]
    sconv = din("sconv", [2, 8, 1024])
    f1i, f1o = din("f1i", [2, 1024, 5632]), din("f1o", [2, 2816, 1024])
    f2i, f2o = din("f2i", [2, 1024, 5632]), din("f2o", [2, 2816, 1024])
    win = din("win", [2, 1024, 9728])
    wao, wco, wo = din("wao", [2, 512, 1024]), din("wco", [2, 1024, 1024]), din("wo", [2, 1024, 1024])
    cst_f = din("cst_f", [128, 1280])
    cst_b = din("cst_b", [128, 768], BF16)
    y = dout("y", [NTOK, 1024])
    kvp = [dout(f"kvp{g}", [2, LB[g], 1024]) for g in range(3)]
    convp = dout("convp", [2, 2, 1024])
    kvs = [dout(f"kvs{g}", [2, 4, LB[g], 1024]) for g in range(3)]
    convs = dout("convs", [2, 8, 1024])

    KTs = dint("KTs", [12, 128, 2048]).ap()
    Vs_d = dint("Vs_d", [48, 128, 512]).ap()
    CHW = [min(CH, PW - k * CH) for k in range(NCH)]
    pack_t = [[dint(f"pack{l}_{k}", [128, CHW[k]]).ap() for k in range(NCH)] for l in range(2)]
    gA_t = [[dint(f"gA{l}_{k}", [512, CHW[k]]).ap() for k in range(NCH)] for l in range(2)]
    b_packs = [[Buf(f"pack{l}_{k}") for k in range(NCH)] for l in range(2)]
    b_gAs = [[Buf(f"gA{l}_{k}") for k in range(NCH)] for l in range(2)]
    halo = dint("halo", [128, PW]).ap()
    aTs = dint("aTs", [4, 128, NP]).ap()
    b_KTs = [Buf(f"KTs{i}") for i in range(12)]
    b_Vs = [Buf(f"Vs{i}") for i in range(48)]
    b_pack, b_gA, b_gB, b_halo = Buf("pack"), Buf("gA"), Buf("gB"), Buf("halo")
    b_aTs = [Buf(f"aTs{i}") for i in range(4)]

    def sb(name, shape, dt):
        return stack.enter_context(nc.sbuf_tensor(name, list(shape), dt))

    xT = sb("xT", [128, 8 * NTOK], F32)
    uT = sb("uT", [128, 8 * NTOK], BF16)
    arena = sb("arena", [128, NPAGE * PG], BF16)
    acc = sb("acc", [128, 2 * NTOK], F32)
    wsl = sb("wsl", [128, 3 * 4096], BF16)
    cf = sb("cf", [128, 1280], F32)
    cb = sb("cb", [128, 768], BF16)
    asT = sb("asT", [128, 64], BF16)
    sm = sb("sm", [128, 256], F32)
    smb = sb("smb", [128, 16], BF16)
    skt = sb("skt", [128, 3 * 512], F32)
    svt = sb("svt", [128, 3 * 512], BF16)
    b_skt = [Buf(f"skt{i}") for i in range(3)]
    b_svt = [Buf(f"svt{i}") for i in range(3)]
    ztb = sb("ztb", [128, 32], BF16)
    b_ztb = Buf("ztb")
    b_x = [[Buf(f"x{c}_{t}") for t in range(5)] for c in range(8)]
    b_u = [[Buf(f"u{c}_{t}") for t in range(5)] for c in range(8)]
    b_pg = [Buf(f"pg{i}") for i in range(NPAGE)]
    b_acc = [Buf("accn"), Buf("accd")]
    b_w = [Buf(f"w{i}") for i in range(3)]
    b_c = Buf("const")
    b_asT = Buf("asT")
    b_sm = Buf("sm")
    b_smb = Buf("smb")
    b_sS, b_sE, b_sP = [Buf("sS0"), Buf("sS1")], [Buf("sE0"), Buf("sE1")], [Buf("sP0"), Buf("sP1")]
    b_sX = Buf("sX")
    psum = [stack.enter_context(nc.psum_tensor(f"ps{i}", [128, 512], F32)) for i in range(8)]
    b_ps = [Buf(f"ps{i}", excl=True) for i in range(8)]
    st = {"ps": 0, "w": 0, "nbank": 8}

    def nps():
        i = st["ps"] % st["nbank"]
        st["ps"] += 1
        return psum[i], b_ps[i]

    def xv(c, t0, tn):
        return xT[:, c * NTOK + t0: c * NTOK + t0 + tn]

    def uv(c, t0, tn, step=1):
        return uT[:, c * NTOK + t0: c * NTOK + t0 + tn * step: step] if step > 1 else uT[:, c * NTOK + t0: c * NTOK + t0 + tn]

    def page(i, n=1):
        return arena[:, i * PG:(i + n) * PG]

    def pagef(i, n=1):
        return arena[:, i * PG:(i + n) * PG].bitcast(F32)

    GN, BG, CW, SEL, IDN, ONESF = 0, 56, 88, 136, 256, 384
    BMASK, EYE8, MKS = 512, 1024, 1040
    MB_N, MB_H, ONESB, IDB = 0, 256, 512, 640

    S.add("sp", lambda e: e.dma_start(out=cf[:], in_=cst_f[:, :]), writes=[b_c], dma=True)
    S.add("sp", lambda e: e.dma_start(out=cb[:], in_=cst_b[:, :]), writes=[b_c], dma=True)

    ident = cf[:, IDN:IDN + 128]
    if os.environ.get("KTOUCH"):
        b_t = Buf("touch")
        for ii, tin in enumerate([f1i, f1o, f2i, f2o, win, wao, wco, wo]):
            S.add("sp", lambda e, tin=tin, ii=ii: e.dma_start(out=sm[0:1, 240 + ii:241 + ii], in_=tin[0, 0:1, 0:1]), writes=[b_t], dma=True)
        for g in range(3):
            S.add("sp", lambda e, g=g: e.dma_start(out=sm[0:1, 250 + g:251 + g], in_=ck[g][0, 0, 0:1, 0:1]), writes=[b_t], dma=True)
        S.add("sp", lambda e: e.dma_start(out=sm[0:1, 253:254], in_=sconv[0, 0:1, 0:1]), writes=[b_t], dma=True)

    accv = acc[:, 0:2 * NTOK]
    NBLK = int(os.environ.get("KNBLK", "17"))
    for blk in range(NBLK):
        r0 = blk * 128
        nr = 128 if blk < 16 else 16
        stg = accv[:, (blk % 2) * 1024:(blk % 2) * 1024 + 1024]
        bs = b_acc[blk % 2]
        S.add("sp", lambda e, stg=stg, r0=r0, nr=nr: e.dma_start(out=stg[0:nr, :], in_=xin[r0:r0 + nr, :]),
              writes=[bs], dma=True)
        ti = blk // 4 if blk < 16 else 4
        for half in range(2):
            ps, bp = nps()
            for cc in range(4):
                c = half * 4 + cc
                S.add("pe", lambda e, ps=ps, stg=stg, c=c, cc=cc, nr=nr: e.transpose(
                    ps[:, cc * 128:cc * 128 + nr], stg[0:nr, c * 128:(c + 1) * 128], ident[0:nr, 0:nr]),
                    reads=[bs, b_c], writes=[bp])
            for cc in range(4):
                c = half * 4 + cc
                S.add("dve" if cc % 2 else "act",
                      (lambda e, ps=ps, c=c, cc=cc, r0=r0, nr=nr: e.tensor_copy(out=xv(c, r0, nr), in_=ps[:, cc * 128:cc * 128 + nr]))
                      if cc % 2 else
                      (lambda e, ps=ps, c=c, cc=cc, r0=r0, nr=nr: e.copy(out=xv(c, r0, nr), in_=ps[:, cc * 128:cc * 128 + nr])),
                      reads=[bp], writes=[b_x[c][ti]])

    pending = []
    for l in range(2 if not os.environ.get("KNOCOPY") else 0):
        for g in range(3):
            for b in range(4):
                for r0 in range(0, LB[g] - 4, 256):
                    rn = min(256, LB[g] - 4 - r0)
                    pending.append(lambda e, l=l, g=g, b=b, r0=r0, rn=rn: e.dma_start(out=kvs[g][l, b, r0:r0 + rn, :], in_=ck[g][l, b, 4 + r0:4 + r0 + rn, :]))

    def issue_copy(dep):
        if pending:
            S.add("sp", pending.pop(0), reads=[dep], dma=True)

    def load_w(wl, kc0, nkc, c0, ncol, off=0, slot=None):
        app = slot is not None
        if slot is None:
            slot = st["w"] % 3
            st["w"] += 1
        view = wsl[:, slot * 4096 + off: slot * 4096 + off + nkc * ncol].rearrange("p (k n) -> p k n", k=nkc)
        src = wl.rearrange("(k p) n -> p k n", p=128)[:, kc0:kc0 + nkc, c0:c0 + ncol]
        S.add("pool", lambda e, view=view, src=src: e.dma_start(out=view, in_=src), writes=[b_w[slot]], dma=True, append=app)
        return view, b_w[slot]

    def rmsnorm(gi, final=False):
        for ti, (t0, tn) in enumerate(TILES):
            ps, bp = nps()
            for c in range(8):
                sq = pagef(c % 2)[:, 0:tn]
                S.add("act", lambda e, sq=sq, c=c, t0=t0, tn=tn: e.activation(out=sq, in_=xv(c, t0, tn), func=AF.Square),
                      reads=[b_x[c][ti]], writes=[b_pg[c % 2]])
                S.add("pe", lambda e, ps=ps, sq=sq, c=c, tn=tn: e.matmul(ps[:, 0:tn], lhsT=cf[:, ONESF:ONESF + 128], rhs=sq,
                                                                       start=(c == 0), stop=(c == 7)),
                      reads=[b_pg[c % 2], b_c], writes=[bp])
            rs = pagef(2)[:, 0:tn]
            S.add("dve", lambda e, ps=ps, rs=rs, tn=tn: e.tensor_scalar(out=rs, in0=ps[:, 0:tn], scalar1=EPS, scalar2=None,
                                                                       op0=ALU.add),
                  reads=[bp], writes=[b_pg[2]])
            S.add("act", lambda e, rs=rs: e.activation(out=rs, in_=rs, func=AF.Sqrt), reads=[b_pg[2]], writes=[b_pg[2]])
            S.add("dve", lambda e, rs=rs: e.reciprocal(out=rs, in_=rs), reads=[b_pg[2]], writes=[b_pg[2]])
            for c in range(8):
                if final:
                    S.add("dve", lambda e, rs=rs, c=c, t0=t0, tn=tn: e.scalar_tensor_tensor(
                        out=xv(c, t0, tn), in0=xv(c, t0, tn), scalar=cf[:, GN + gi * 8 + c:GN + gi * 8 + c + 1], in1=rs,
                        op0=ALU.mult, op1=ALU.mult), reads=[b_x[c][ti], b_pg[2], b_c], writes=[b_x[c][ti]])
                else:
                    S.add("dve", lambda e, rs=rs, c=c, t0=t0, tn=tn: e.scalar_tensor_tensor(
                        out=uv(c, t0, tn), in0=xv(c, t0, tn), scalar=cf[:, GN + gi * 8 + c:GN + gi * 8 + c + 1], in1=rs,
                        op0=ALU.mult, op1=ALU.mult), reads=[b_x[c][ti], b_pg[2], b_c], writes=[b_u[c][ti]])

    def ffn(wi, wo_, gi):
        rmsnorm(gi)
        for half in range(2):
            for jj in range(11):
                j = half * 11 + jj
                if jj % 2 == 0:
                    ncol = 256 if jj < 10 else 128
                    wv, bw = load_w(wi, 0, 8, j * 128, ncol, 0)
                    slot = (st["w"] - 1) % 3
                    wv2, _ = load_w(wi, 0, 8, 2816 + j * 128, ncol, 2048, slot=slot)
                co = (jj % 2) * 128
                for ti, (t0, tn) in enumerate(TILES):
                    pg_, bg_ = nps()
                    pu_, bu_ = nps()
                    for k in range(8):
                        S.add("pe", lambda e, p=pg_, wv=wv, k=k, co=co, t0=t0, tn=tn: e.matmul(
                            p[:, 0:tn], lhsT=wv[:, k, co:co + 128], rhs=uv(k, t0, tn), start=(k == 0), stop=(k == 7)),
                            reads=[bw, b_u[k][ti]], writes=[bg_])
                    for k in range(8):
                        S.add("pe", lambda e, p=pu_, wv=wv2, k=k, co=co, t0=t0, tn=tn: e.matmul(
                            p[:, 0:tn], lhsT=wv[:, k, co:co + 128], rhs=uv(k, t0, tn), start=(k == 0), stop=(k == 7)),
                            reads=[bw, b_u[k][ti]], writes=[bu_])
                    hv = page(jj)[:, t0:t0 + tn]
                    sg = pagef(11)[:, 0:tn]
                    S.add("act", lambda e, p=pg_, sg=sg, tn=tn: e.activation(out=sg, in_=p[:, 0:tn], func=AF.Silu),
                          reads=[bg_], writes=[b_pg[11]])
                    S.add("dve", lambda e, p=pu_, sg=sg, hv=hv, tn=tn: e.tensor_tensor(out=hv, in0=p[:, 0:tn], in1=sg, op=ALU.mult),
                          reads=[bu_, b_pg[11]], writes=[b_pg[jj]])
                issue_copy(b_pg[jj])
            for oc in range(8):
                wv, bw = load_w(wo_, half * 11, 11, oc * 128, 128)
                for ti, (t0, tn) in enumerate(TILES):
                    ps, bp = nps()
                    for jj in range(11):
                        S.add("pe", lambda e, p=ps, wv=wv, jj=jj, t0=t0, tn=tn: e.matmul(
                            p[:, 0:tn], lhsT=wv[:, jj, :], rhs=page(jj)[:, t0:t0 + tn], start=(jj == 0), stop=(jj == 10)),
                            reads=[bw, b_pg[jj]], writes=[bp])
                    S.add("dve", lambda e, p=ps, oc=oc, t0=t0, tn=tn: e.scalar_tensor_tensor(
                        out=xv(oc, t0, tn), in0=p[:, 0:tn], scalar=0.5, in1=xv(oc, t0, tn), op0=ALU.mult, op1=ALU.add),
                        reads=[bp, b_x[oc][ti]], writes=[b_x[oc][ti]])

    def proj_fm(wv, bw, m0, M, ti, ps, bp):
        t0, tn = TILES[ti]
        for k in range(8):
            S.add("pe", lambda e, k=k: e.matmul(ps[0:M, 0:tn], lhsT=wv[:, k, m0:m0 + M], rhs=uv(k, t0, tn),
                                               start=(k == 0), stop=(k == 7)),
                  reads=[bw, b_u[k][ti]], writes=[bp])

    def deint(apv, d, t0):
        return apv[:, 0:NP].rearrange("p (r m) -> p r m", r=d)[:, :, t0 // d:(t0 + 512) // d]

    def mixer(l):
        wl = win[l]
        pack, gA = pack_t[l], gA_t[l]

        def pk(off, n):
            k = off // CH
            assert (off + n - 1) // CH == k
            return pack[k][:, off - k * CH: off - k * CH + n], b_packs[l][k]
        rmsnorm(l * 3 + 1)
        Qs, Ks, Vsm = page(0), pagef(2, 2), pagef(4, 2)
        for g in range(3):
            d = DIL[g]
            nb = 16 // d
            wq, bwq = load_w(wl, 0, 8, g * 512, 512)
            ps, bp = nps()
            for k in range(8):
                S.add("pe", lambda e, ps=ps, wq=wq, k=k: e.matmul(ps[0:16, :], lhsT=uv(k, NP, 16), rhs=wq[:, k, :],
                                                               start=(k == 0), stop=(k == 7)), reads=[bwq, b_u[k][4]], writes=[bp])
            S.add("act", lambda e, ps=ps, g=g: e.activation(out=Qs[0:16, g * 512:(g + 1) * 512], in_=ps[0:16, :], func=AF.Copy, scale=0.125),
                  reads=[bp], writes=[b_pg[0]])
            wk, bwk = load_w(wl, 0, 8, 1536 + g * 512, 512)
            wv_, bwv = load_w(wl, 0, 8, 3072 + g * 512, 512)
            for (wx, bwx, dst, bd) in ((wk, bwk, Ks, (b_pg[2], b_pg[3])), (wv_, bwv, Vsm, (b_pg[4], b_pg[5]))):
                ps, bp = nps()
                for k in range(8):
                    S.add("pe", lambda e, ps=ps, wx=wx, k=k: e.matmul(ps[0:16, :], lhsT=uv(k, NP, 16), rhs=wx[:, k, :],
                                                                   start=(k == 0), stop=(k == 7)), reads=[bwx, b_u[k][4]], writes=[bp])
                S.add("dve", lambda e, ps=ps, dst=dst, g=g: e.tensor_copy(out=dst[0:16, g * 512:(g + 1) * 512], in_=ps[0:16, :]),
                      reads=[bp], writes=list(bd))
            for kv, src in ((0, Ks), (1, Vsm)):
                for b in range(4):
                    S.add("sp", lambda e, g=g, kv=kv, src=src, b=b: e.dma_start(
                        out=kvs[g][l, b, LB[g] - 4:LB[g], kv * 512:(kv + 1) * 512], in_=src[b * 4:b * 4 + 4, g * 512:(g + 1) * 512]),
                        reads=[b_pg[2 + 2 * kv], b_pg[3 + 2 * kv]], dma=True)
            for nbk in range(16):
                tail = (nbk >= 16 - d)
                for isk in ((0, 1) if tail else (0,)):
                    wx, bwx = (wk, bwk) if isk else (wv_, bwv)
                    ps, bp = nps()
                    for k in range(8):
                        S.add("pe", lambda e, ps=ps, wx=wx, k=k, nbk=nbk: e.matmul(
                            ps[:, :], lhsT=uv(k, nbk * 128, 128), rhs=wx[:, k, :], start=(k == 0), stop=(k == 7)),
                            reads=[bwx, b_u[k][nbk // 4]], writes=[bp])
                    if not isk:
                        vst = page(6 + nbk % 2)[:, 0:512]
                        S.add("act", lambda e, ps=ps, vst=vst: e.copy(out=vst, in_=ps[:, :]), reads=[bp], writes=[b_pg[6 + nbk % 2]])
                        S.add("sp", lambda e, vst=vst, g=g, nbk=nbk: e.dma_start(out=Vs_d[g * 16 + nbk], in_=vst),
                              reads=[b_pg[6 + nbk % 2]], writes=[b_Vs[g * 16 + nbk]], dma=True)
                    if tail:
                        fst = pagef(8 + (nbk + isk) % 2)[:, 0:512]
                        bfs = b_pg[8 + (nbk + isk) % 2]
                        row0 = nbk * 128 - (NP - LB[g])
                        S.add("dve", lambda e, ps=ps, fst=fst: e.tensor_copy(out=fst, in_=ps[:, :]), reads=[bp], writes=[bfs])
                        S.add("sp", lambda e, fst=fst, g=g, row0=row0, isk=isk: e.dma_start(
                            out=kvp[g][l, row0:row0 + 128, (1 - isk) * 512:(2 - isk) * 512], in_=fst), reads=[bfs], dma=True)
            vnat = Vs_d[g * 16:(g + 1) * 16].rearrange("b p c -> (b p) c")
            for r in range(d):
                pkv, bpk = pk(VOFF[g] + r * 512, 512)
                st0 = NP - 128 * d + r
                S.add("sp", lambda e, pkv=pkv, st0=st0, d=d, vnat=vnat: e.dma_start(out=pkv, in_=vnat[st0:NP:d, :]),
                      reads=b_Vs[g * 16 + 16 - d:(g + 1) * 16], writes=[bpk], dma=True, append=True)
        if STAGE < 4 + 10 * l:
            return
        if STAGE < 5 + 10 * l:
            return
        for g in range(3):
            d = DIL[g]
            for pr in range(4):
                wv, bw = load_w(wl, 0, 8, 1536 + g * 512 + pr * 128, 128)
                stg = page(10 + pr % 2)
                bst = b_pg[10 + pr % 2]
                for ti in range(4):
                    ps, bp = nps()
                    proj_fm(wv, bw, 0, 128, ti, ps, bp)
                    t0 = TILES[ti][0]
                    S.add("dve", lambda e, ps=ps, stg=stg, d=d, t0=t0: e.tensor_copy(
                        out=deint(stg, d, t0), in_=ps[:, :].rearrange("p (j r) -> p r j", r=d)), reads=[bp], writes=[bst])
                S.add("sp", lambda e, stg=stg, g=g, pr=pr: e.dma_start(out=KTs[g * 4 + pr], in_=stg[:, 0:NP]),
                      reads=[bst], writes=[b_KTs[g * 4 + pr]], dma=True)
                nb = 16 // d
                src = stg[:, 0:NP].rearrange("p (r n i) -> p r n i", r=d, n=nb)[:, :, nb - 1, :]
                pkv, bpk = pk(KOFF[(g, pr)], HK[g])
                dst = pkv.rearrange("p (r i) -> p r i", r=d)
                S.add("sp", lambda e, src=src, dst=dst: e.dma_start(out=dst, in_=src), reads=[bst], writes=[bpk], dma=True, append=True)
        zt = sm[:, 0:16]
        for c in range(8):
            wv, bw = load_w(wl, 0, 8, 5632 + c * 128, 128, 0)
            slot = (st["w"] - 1) % 3
            wv2, _ = load_w(wl, 0, 8, 6656 + c * 128, 128, 1024, slot=slot)
            ps, bp = nps()
            for k in range(8):
                S.add("pe", lambda e, ps=ps, wv=wv, k=k: e.matmul(ps[:, 0:2], lhsT=wv[:, k, :], rhs=uv(k, NP - 2, 2), start=(k == 0), stop=(k == 7)),
                      reads=[bw, b_u[k][3]], writes=[bp])
            for k in range(8):
                S.add("pe", lambda e, ps=ps, wv=wv2, k=k: e.matmul(ps[:, 8:10], lhsT=wv[:, k, :], rhs=uv(k, NP - 2, 2), start=(k == 0), stop=(k == 7)),
                      reads=[bw, b_u[k][3]], writes=[bp])
            S.add("act", lambda e, ps=ps: e.copy(out=sm[:, 16:18], in_=ps[:, 8:10]), reads=[bp], writes=[b_sm])
            S.add("dve", lambda e, ps=ps, c=c: e.tensor_tensor(out=zt[:, c * 2:c * 2 + 2], in0=ps[:, 0:2], in1=sm[:, 16:18], op=ALU.mult),
                  reads=[bp, b_sm], writes=[b_sm])
        S.add("dve", lambda e: e.tensor_copy(out=ztb[:, 0:16], in_=zt), reads=[b_sm], writes=[b_ztb])
        pkz, bpkz = pk(ZOFF, 16)
        S.add("sp", lambda e: e.dma_start(out=pkz, in_=ztb[:, 0:16]), reads=[b_ztb], writes=[bpkz], dma=True, append=True)
        ps, bp = nps()
        S.add("pe", lambda e, ps=ps: e.transpose(ps[0:16, 0:128], zt, ident), reads=[b_sm, b_c], writes=[bp])
        S.add("dve", lambda e, ps=ps: e.tensor_copy(out=sm[0:16, 32:160], in_=ps[0:16, 0:128]), reads=[bp], writes=[b_sm])
        for c in range(8):
            S.add("sp", lambda e, c=c: e.dma_start(out=convp[l, :, c * 128:(c + 1) * 128], in_=sm[2 * c:2 * c + 2, 32:160]), reads=[b_sm], dma=True)
        if STAGE < 6 + 10 * l:
            return
        for k in range(NCH):
            S.add("pool", lambda e, k=k: e.collective_compute("AllGather", ALU.bypass, replica_groups=[[0, 1, 2, 3], [4, 5, 6, 7]],
                                                           ins=[pack[k].opt()], outs=[gA[k].opt()]), reads=[b_packs[l][k]], writes=[b_gAs[l][k]])
        sample_attention(l, Qs, Ks, Vsm)
        c0 = 0
        i = 0
        while c0 < PW:
            cn = min(2048, PW - c0)
            k = c0 // CH
            ck0 = c0 - k * CH
            pgs = [6, 7, 8] if i % 2 == 0 else [9, 10, 11]
            pv = [page(p)[:, 0:cn] for p in pgs]
            bv = [b_pg[p] for p in pgs]
            for j in range(3):
                S.add("sp", lambda e, j=j, pv=pv, k=k, ck0=ck0, cn=cn: e.dma_start(out=pv[j], in_=gA[k][j * 128:(j + 1) * 128, ck0:ck0 + cn]),
                      reads=[b_gAs[l][k]], writes=[bv[j]], dma=True)
            S.add("dve", lambda e, pv=pv: e.tensor_scalar(out=pv[0], in0=pv[0], scalar1=cf[:, SEL:SEL + 1], scalar2=None, op0=ALU.mult),
                  reads=[bv[0], b_c], writes=[bv[0]])
            S.add("dve", lambda e, pv=pv: e.scalar_tensor_tensor(out=pv[0], in0=pv[1], scalar=cf[:, SEL + 1:SEL + 2], in1=pv[0],
                                                                op0=ALU.mult, op1=ALU.add), reads=[bv[0], bv[1], b_c], writes=[bv[0]])
            S.add("dve", lambda e, pv=pv: e.scalar_tensor_tensor(out=pv[0], in0=pv[2], scalar=cf[:, SEL + 3:SEL + 4], in1=pv[0],
                                                                op0=ALU.mult, op1=ALU.add), reads=[bv[0], bv[2], b_c], writes=[bv[0]])
            S.add("sp", lambda e, pv=pv, c0=c0, cn=cn: e.dma_start(out=halo[:, c0:c0 + cn], in_=pv[0]), reads=[bv[0]], writes=[b_halo], dma=True, append=True)
            c0 += cn
            i += 1
        if STAGE < 7 + 10 * l:
            return
        for pr in range(4):
            for g in range(3):
                attention(l, pr, g, wl)
            stg = page(10 + pr % 2)
            bst = b_pg[10 + pr % 2]
            S.add("dve", lambda e: e.reciprocal(out=acc[:, NTOK:NTOK + NP], in_=acc[:, NTOK:NTOK + NP]), reads=[b_acc[1]], writes=[b_acc[1]])
            S.add("dve", lambda e, stg=stg: e.tensor_tensor(out=stg[:, 0:NP], in0=acc[:, 0:NP], in1=acc[:, NTOK:NTOK + NP], op=ALU.mult),
                  reads=b_acc, writes=[bst])
            S.add("sp", lambda e, stg=stg, pr=pr: e.dma_start(out=aTs[pr], in_=stg[:, 0:NP]), reads=[bst], writes=[b_aTs[pr]], dma=True)
        if STAGE < 8 + 10 * l:
            return
        out_phase(l)

    def attention(l, pr, g, wl):
        d = DIL[g]
        nb = 16 // d
        s = (pr * 3 + g) % 2
        P0 = 5 * s
        qt, kto, kth, vo, vh = page(P0), page(P0 + 1), page(P0 + 2), page(P0 + 3), page(P0 + 4)
        bq, bko, bkh, bvo, bvh = (b_pg[P0 + i] for i in range(5))
        S.add("sp", lambda e: e.dma_start(out=kto[:, 0:NP], in_=KTs[g * 4 + pr]), reads=[b_KTs[g * 4 + pr]], writes=[bko], dma=True)
        S.add("sp", lambda e: e.dma_start(out=kth[:, 0:HK[g]], in_=halo[:, KOFF[(g, pr)]:KOFF[(g, pr)] + HK[g]]),
              reads=[b_halo], writes=[bkh], dma=True)
        vnat = Vs_d[g * 16:(g + 1) * 16].rearrange("b p c -> (b p) c")
        for r in range(d):
            S.add("sp", lambda e, r=r: e.dma_start(out=vo[:, r * nb * 128:(r + 1) * nb * 128].rearrange("p (n c) -> p n c", c=128),
                                                  in_=vnat[r:NP:d, pr * 128:(pr + 1) * 128].rearrange("(n i) c -> i n c", i=128)),
                  reads=b_Vs[g * 16:(g + 1) * 16], writes=[bvo], dma=True, append=(r > 0))
        S.add("sp", lambda e: e.dma_start(out=vh[:, 0:NHB[g] * 128].rearrange("p (b c) -> p b c", c=128),
                                          in_=halo[:, VOFF[g]:VOFF[g] + NHB[g] * 512].rearrange("p (b c) -> p b c", c=512)[:, :, pr * 128:(pr + 1) * 128]),
              reads=[b_halo], writes=[bvh], dma=True)
        wv, bw = load_w(wl, 0, 8, g * 512 + pr * 128, 128)
        for ti in range(4):
            ps, bp = nps()
            proj_fm(wv, bw, 0, 128, ti, ps, bp)
            t0 = TILES[ti][0]
            S.add("act", lambda e, ps=ps, t0=t0: e.activation(out=deint(qt, d, t0), in_=ps[:, :].rearrange("p (j r) -> p r j", r=d),
                                                             func=AF.Copy, scale=0.125), reads=[bp], writes=[bq])
        ones = cb[:, ONESB:ONESB + 128]
        for hh in range(2):
            hp = slice(64 * hh, 64 * hh + 64)
            for qd in range(4):
                pn, bpn = nps()
                pd, bpd = nps()
                pts = []
                for half in range(2):
                    ps, bp = nps()
                    for uu in range(2):
                        bi = qd * 4 + half * 2 + uu
                        r, n = bi // nb, bi % nb
                        qv = qt[hp, bi * 128:(bi + 1) * 128]
                        if n > 0:
                            kp, bkp = kto[hp, (bi - 1) * 128:bi * 128], bko
                        else:
                            kp, bkp = kth[hp, r * 128:(r + 1) * 128], bkh
                        kc_ = kto[hp, bi * 128:(bi + 1) * 128]
                        mo = MB_H if n == 0 else MB_N
                        S.add("pe", lambda e, ps=ps, uu=uu, mo=mo: e.matmul(ps[:, uu * 256:uu * 256 + 256], lhsT=cb[:, IDB:IDB + 128], rhs=cb[:, mo:mo + 256],
                                                                           start=True, stop=False, skip_group_check=True), reads=[b_c], writes=[bp])
                        S.add("pe", lambda e, ps=ps, kp=kp, qv=qv, uu=uu: e.matmul(ps[:, uu * 256:uu * 256 + 128], lhsT=kp, rhs=qv, start=False, stop=False,
                                                                                  skip_group_check=True), reads=[bkp, bq], writes=[bp])
                        S.add("pe", lambda e, ps=ps, kc_=kc_, qv=qv, uu=uu: e.matmul(ps[:, uu * 256 + 128:uu * 256 + 256], lhsT=kc_, rhs=qv, start=False, stop=True,
                                                                                    skip_group_check=True), reads=[bko, bq], writes=[bp])
                    pt = page(10 + half)[:, 0:512]
                    bpt = b_pg[10 + half]
                    S.add("act", lambda e, ps=ps, pt=pt: e.activation(out=pt, in_=ps[:, :], func=AF.Exp), reads=[bp], writes=[bpt])
                    pts.append((pt, bpt))
                for half in range(2):
                    pt, bpt = pts[half]
                    for uu in range(2):
                        bi = qd * 4 + half * 2 + uu
                        r, n = bi // nb, bi % nb
                        u4 = half * 2 + uu
                        if n > 0:
                            vp, bvp = vo[:, (bi - 1) * 128:bi * 128], bvo
                        else:
                            vp, bvp = vh[:, r * 128:(r + 1) * 128], bvh
                        vc = vo[:, bi * 128:(bi + 1) * 128]
                        o_n = pn[:, u4 * 128:(u4 + 1) * 128]
                        o_d = pd[:, u4 * 128:(u4 + 1) * 128]
                        S.add("pe", lambda e, o_n=o_n, vp=vp, pt=pt, uu=uu: e.matmul(o_n, lhsT=vp, rhs=pt[:, uu * 256:uu * 256 + 128], start=True, stop=False),
                              reads=[bvp, bpt], writes=[bpn])
                        S.add("pe", lambda e, o_n=o_n, vc=vc, pt=pt, uu=uu: e.matmul(o_n, lhsT=vc, rhs=pt[:, uu * 256 + 128:uu * 256 + 256], start=False, stop=True),
                              reads=[bvo, bpt], writes=[bpn])
                        S.add("pe", lambda e, o_d=o_d, pt=pt, uu=uu: e.matmul(o_d, lhsT=ones, rhs=pt[:, uu * 256:uu * 256 + 128], start=True, stop=False),
                              reads=[b_c, bpt], writes=[bpd])
                        S.add("pe", lambda e, o_d=o_d, pt=pt, uu=uu: e.matmul(o_d, lhsT=ones, rhs=pt[:, uu * 256 + 128:uu * 256 + 256], start=False, stop=True),
                              reads=[b_c, bpt], writes=[bpd])
                for which, (pp, bpp) in enumerate(((pn, bpn), (pd, bpd))):
                    base = which * NTOK
                    if g == 0:
                        dst = acc[hp, base + qd * 512: base + qd * 512 + 512]
                        src = pp[hp, :]
                    elif g == 1:
                        dst = acc[hp, base + qd: base + NP: 4]
                        src = pp[hp, :]
                    else:
                        dst = acc[hp, base: base + NP].rearrange("p (m r) -> p m r", r=16)[:, :, qd * 4:qd * 4 + 4]
                        src = pp[hp, :].rearrange("p (r m) -> p m r", r=4)
                    if g == 0:
                        S.add("act", lambda e, dst=dst, src=src: e.copy(out=dst, in_=src), reads=[bpp], writes=[b_acc[which]])
                    else:
                        S.add("dve", lambda e, dst=dst, src=src: e.tensor_tensor(out=dst, in0=dst, in1=src, op=ALU.add),
                              reads=[bpp, b_acc[which]], writes=[b_acc[which]])

    def sample_attention(l, Qs, Ks, Vsm):
        st["nbank"] = 6
        prods = [pagef(8)[:, 0:512], pagef(11)[:, 0:512]]
        b_prod = [b_pg[8], b_pg[11]]
        vn_b = page(9)[:, 0:1536]
        S.add("dve", lambda e: e.tensor_copy(out=vn_b[0:16, :], in_=Vsm[0:16, 0:1536]), reads=[b_pg[4], b_pg[5]], writes=[b_pg[9]])
        pa_, bpa = psum[7], b_ps[7]
        po, bpo = psum[6], b_ps[6]
        pdn, bpdn = psum[7][:, 64:128], b_ps[7]
        steps = [(bt, g, ks) for bt in range(16) for g in range(3) for ks in range(2)]
        info = {}

        def prep(i):
            bt, g, ks = steps[i]
            b, t = bt // 4, bt % 4
            d = DIL[g]
            NK = 128 if ks == 0 else 16
            if ks == 0:
                si = (i // 2) % 3
                kc_t = skt[:, si * 512:(si + 1) * 512]
                vc_b = svt[:, si * 512:(si + 1) * 512]
                S.add("sp", lambda e: e.dma_start(out=kc_t, in_=ck[g][l, b, (t if d > 1 else 0):LB[g]:d, 0:512]), writes=[b_skt[si]], dma=True)
                S.add("pool", lambda e: e.dma_start(out=vc_b, in_=ck[g][l, b, (t if d > 1 else 0):LB[g]:d, 512:1024]), writes=[b_svt[si]], dma=True)
                kt_, bk_, vt_, bv_ = kc_t, [b_skt[si]], vc_b, [b_svt[si]]
                mcol = cf[:, MKS + g * 4 + t:MKS + g * 4 + t + 1]
            else:
                kt_, bk_, vt_, bv_ = Ks[0:16, g * 512:(g + 1) * 512], [b_pg[2], b_pg[3]], vn_b[0:16, g * 512:(g + 1) * 512], [b_pg[9]]
                mcol = cf[0:16, MKS + 16 + (g * 16 + bt):MKS + 16 + (g * 16 + bt) + 1]
            pq, bpq = nps()
            S.add("pe", lambda e: e.matmul(pq[0:NK, :], lhsT=cb[0:16, IDB + bt:IDB + bt + 1].to_broadcast([16, NK]),
                                           rhs=Qs[0:16, g * 512:(g + 1) * 512], start=True, stop=True), reads=[b_c, b_pg[0]], writes=[bpq])
            info[i] = (NK, kt_, bk_, vt_, bv_, mcol, pq, bpq)

        prep(0)
        for i, (bt, g, ks) in enumerate(steps):
            NK, kt_, bk_, vt_, bv_, mcol, pq, bpq = info.pop(i)
            par = i % 2
            prod, bprod = prods[par], b_prod[par]
            sS, sE, sP = sm[:, 192 + 8 * par:200 + 8 * par], sm[:, 208 + 8 * par:216 + 8 * par], smb[:, 8 * par:8 * par + 8]
            bS, bE, bP = b_sS[par], b_sE[par], b_sP[par]
            step = i % 6
            S.add("dve", lambda e, pq=pq, NK=NK, kt_=kt_, prod=prod: e.tensor_tensor(out=prod[0:NK, :], in0=kt_[0:NK, :], in1=pq[0:NK, :], op=ALU.mult),
                  reads=[bpq] + bk_, writes=[bprod])
            S.add("dve", lambda e, NK=NK, prod=prod, sS=sS: e.tensor_reduce(out=sS[0:NK, :], in_=prod[0:NK, :].rearrange("p (h e) -> p h e", e=64),
                                                                         axis=AX.X, op=ALU.add), reads=[bprod], writes=[bS])
            S.add("act", lambda e, NK=NK, sS=sS, sE=sE: e.activation(out=sE[0:NK, :], in_=sS[0:NK, :], func=AF.Exp), reads=[bS], writes=[bE])
            S.add("dve", lambda e, NK=NK, mcol=mcol, sE=sE, sP=sP: e.tensor_scalar(out=sP[0:NK, :], in0=sE[0:NK, :], scalar1=mcol[0:NK, :], scalar2=None,
                                                                                 op0=ALU.mult), reads=[bE, b_c], writes=[bP])
            if i + 1 < len(steps):
                prep(i + 1)
            S.add("pe", lambda e, NK=NK, vt_=vt_, step=step, sP=sP: e.matmul(po[0:8, 0:512], lhsT=sP[0:NK, :], rhs=vt_[0:NK, :],
                                                                           start=(step == 0), stop=(step == 5)), reads=[bP] + bv_, writes=[bpo])
            S.add("pe", lambda e, NK=NK, step=step, sP=sP: e.matmul(pdn[0:8, 0:8], lhsT=sP[0:NK, :], rhs=cb[0:NK, ONESB:ONESB + 8],
                                                                  start=(step == 0), stop=(step == 5)), reads=[bP, b_c], writes=[bpdn])
            if step != 5:
                continue
            ex = pagef(10)[:, 0:512]
            S.add("dve", lambda e: e.tensor_tensor(out=ex[0:8, 0:512], in0=po[0:8, 0:512], in1=cf[0:8, BMASK:BMASK + 512], op=ALU.mult),
                  reads=[bpo, b_c], writes=[b_pg[10]])
            S.add("dve", lambda e: e.tensor_tensor(out=sm[0:8, 224:232], in0=pdn[0:8, 0:8], in1=cf[0:8, EYE8:EYE8 + 8], op=ALU.mult),
                  reads=[bpdn, b_c], writes=[b_sX])
            pe1, bpe1 = nps()
            S.add("pe", lambda e, pe1=pe1: e.matmul(pe1[0:1, 0:512], lhsT=cf[0:8, EYE8 + 8:EYE8 + 9], rhs=ex[0:8, 0:512], start=True, stop=True),
                  reads=[b_pg[10], b_c], writes=[bpe1])
            pe2, bpe2 = nps()
            S.add("pe", lambda e, pe2=pe2: e.matmul(pe2[0:1, 0:8], lhsT=cf[0:8, EYE8 + 8:EYE8 + 9], rhs=sm[0:8, 224:232], start=True, stop=True),
                  reads=[b_sX, b_c], writes=[bpe2])
            S.add("dve", lambda e, pe2=pe2: e.reciprocal(out=sm[0:1, 232:240], in_=pe2[0:1, 0:8]), reads=[bpe2], writes=[b_sX])
            arow = pagef(10)[0:1, 520:1032]
            S.add("dve", lambda e, pe1=pe1, arow=arow: e.tensor_tensor(out=arow.rearrange("p (h e) -> p h e", e=64), in0=pe1[0:1, 0:512].rearrange("p (h e) -> p h e", e=64),
                                                                      in1=sm[0:1, 232:240].rearrange("p (h o) -> p h o", o=1).to_broadcast([1, 8, 64]), op=ALU.mult),
                  reads=[bpe1, b_sX], writes=[b_pg[10]])
            for pr in range(4):
                S.add("pe", lambda e, pr=pr, bt=bt, arow=arow: e.matmul(pa_[:, pr * 16 + bt:pr * 16 + bt + 1], lhsT=arow[0:1, pr * 128:(pr + 1) * 128],
                                                                       rhs=cf[0:1, EYE8:EYE8 + 1], start=True, stop=True), reads=[b_pg[10], b_c], writes=[bpa])
        S.add("dve", lambda e: e.tensor_copy(out=asT[:, 0:64], in_=pa_[:, 0:64]), reads=[bpa], writes=[b_asT])
        st["nbank"] = 8

    def out_phase(l):
        wl = win[l]
        parts = [(0, 1024), (1024, 1040)]
        for pi, (p0, pn) in enumerate(parts):
            out_part(l, wl, pi, p0, pn)

    def out_part(l, wl, pi, p0, pn):
        if True:
            subs = [(0, 512), (512, 512)] if pi == 0 else [(0, 512), (512, 512), (1024, 16)]
            tis = {0: [0, 1], 1: [2, 3, 4]}[pi]
            aT = arena[:, 8 * PG:8 * PG + 4 * pn].rearrange("p (c n) -> p c n", c=4)
            cT = arena[:, 0:8 * pn].rearrange("p (c n) -> p c n", c=8)
            mT = arena[:, 4 * PG:4 * PG + 8 * pn].rearrange("p (c n) -> p c n", c=8)
            b_aT, b_cT, b_mT = [b_pg[8], b_pg[9]], [b_pg[0], b_pg[1], b_pg[2], b_pg[3]], [b_pg[4], b_pg[5], b_pg[6], b_pg[7]]
            for pr in range(4):
                S.add("sp", lambda e, pr=pr, aT=aT, p0=p0: e.dma_start(out=aT[:, pr, 0:1024], in_=aTs[pr, :, p0:p0 + 1024]),
                      reads=[b_aTs[pr]], writes=b_aT, dma=True, append=(pr > 0))
            if pi == 1:
                S.add("dve", lambda e, aT=aT: e.tensor_copy(out=aT[:, :, 1024:1040], in_=asT[:, 0:64].rearrange("p (c n) -> p c n", c=4)),
                      reads=[b_asT], writes=b_aT, append=True)
            zs = sm[:, 160:184].rearrange("p (b j) -> p b j", j=6)
            scst = acc[0:8, 0:1024]
            cvst = acc[0:8, NTOK:NTOK + 128]
            zc = pagef(10)
            zb = pagef(11)
            for c in range(8):
                wb_, bwb = load_w(wl, 0, 8, 4608 + c * 128, 128, 0)
                slot = (st["w"] - 1) % 3
                wc_, _ = load_w(wl, 0, 8, 5632 + c * 128, 128, 1024, slot=slot)
                wh_, _ = load_w(wl, 0, 8, 6656 + c * 128, 128, 2048, slot=slot)
                w0, w1, w2 = (cf[:, CW + (l * 3 + j) * 8 + c:CW + (l * 3 + j) * 8 + c + 1] for j in range(3))
                if pi == 0:
                    if c == 0:
                        S.add("sp", lambda e: e.dma_start(out=ztb[:, 16:32], in_=halo[:, ZOFF:ZOFF + 16]), reads=[b_halo], writes=[b_ztb], dma=True)
                    S.add("dve", lambda e, c=c: e.tensor_scalar(out=zc[:, 0:2], in0=ztb[:, 16 + 2 * c:18 + 2 * c], scalar1=cf[:, SEL + 2:SEL + 3], scalar2=None,
                                                               op0=ALU.mult), reads=[b_ztb, b_c], writes=[b_pg[10]])
                elif pi == 1:
                    S.add("dve", lambda e, c=c: e.tensor_copy(out=zc[:, 0:2], in_=sm[:, 64 + 2 * c:66 + 2 * c]), reads=[b_sm], writes=[b_pg[10]])
                for (s0, sn) in subs:
                    ti = tis[s0 // 512]
                    t0 = p0 + s0
                    pcs = []
                    for wv in (wb_, wc_, wh_):
                        ps, bp = nps()
                        for k in range(8):
                            S.add("pe", lambda e, ps=ps, wv=wv, k=k, t0=t0, sn=sn: e.matmul(ps[:, 0:sn], lhsT=wv[:, k, :], rhs=uv(k, t0, sn), start=(k == 0), stop=(k == 7)),
                                  reads=[bwb, b_u[k][ti]], writes=[bp])
                        pcs.append((ps, bp))
                    (pb_, bpb), (pc_, bpc), (ph_, bph) = pcs
                    S.add("act", lambda e, ph_=ph_, sn=sn: e.copy(out=zb[:, 0:sn], in_=ph_[:, 0:sn]), reads=[bph], writes=[b_pg[11]])
                    is_s = (t0 == NP)
                    bz = b_sm if is_s else b_pg[10]
                    if not is_s:
                        S.add("dve", lambda e, pc_=pc_, s0=s0, sn=sn: e.tensor_tensor(out=zc[:, 2 + s0:2 + s0 + sn], in0=pc_[:, 0:sn], in1=zb[:, 0:sn], op=ALU.mult),
                              reads=[bpc, b_pg[11]], writes=[b_pg[10]])
                    else:
                        S.add("dve", lambda e, pc_=pc_: e.tensor_tensor(out=zs[:, :, 2:6],
                                                                       in0=pc_[:, 0:16].rearrange("p (b j) -> p b j", j=4),
                                                                       in1=zb[:, 0:16].rearrange("p (b j) -> p b j", j=4), op=ALU.mult),
                              reads=[bpc, b_pg[11]], writes=[b_sm])
                        if c == 0:
                            S.add("sp", lambda e: e.dma_start(out=scst, in_=sconv[l, :, :]), writes=[b_acc[0]], dma=True)
                        srcst = scst[:, c * 128:(c + 1) * 128]
                        pst, bpst = nps()
                        S.add("pe", lambda e, pst=pst, srcst=srcst: e.transpose(pst[:, 0:8], srcst, ident[0:8, 0:8]), reads=[b_acc[0], b_c], writes=[bpst])
                        S.add("dve", lambda e, pst=pst: e.tensor_copy(out=zs[:, :, 0:2], in_=pst[:, 0:8].rearrange("p (b j) -> p b j", j=2)),
                              reads=[bpst], writes=[b_sm])
                        z0, z1, z2 = zs[:, :, 0:4], zs[:, :, 1:5], zs[:, :, 2:6]
                        yv = zb[:, 16:32].rearrange("p (b j) -> p b j", j=4)
                        outc = cT[:, c, 1024:1040].rearrange("p (b j) -> p b j", j=4)
                        pbv = pb_[:, 0:16].rearrange("p (b j) -> p b j", j=4)
                    if not is_s:
                        z0, z1, z2 = zc[:, s0:s0 + sn], zc[:, s0 + 1:s0 + 1 + sn], zc[:, s0 + 2:s0 + 2 + sn]
                        yv = zb[:, 512:512 + sn]
                        outc = cT[:, c, s0:s0 + sn]
                        pbv = pb_[:, 0:sn]
                    S.add("dve", lambda e, yv=yv, z0=z0, w0=w0: e.tensor_scalar(out=yv, in0=z0, scalar1=w0, scalar2=None, op0=ALU.mult),
                          reads=[bz, b_c], writes=[b_pg[11]])
                    S.add("dve", lambda e, yv=yv, z1=z1, w1=w1: e.scalar_tensor_tensor(out=yv, in0=z1, scalar=w1, in1=yv, op0=ALU.mult, op1=ALU.add),
                          reads=[bz, b_pg[11], b_c], writes=[b_pg[11]])
                    S.add("dve", lambda e, yv=yv, z2=z2, w2=w2: e.scalar_tensor_tensor(out=yv, in0=z2, scalar=w2, in1=yv, op0=ALU.mult, op1=ALU.add),
                          reads=[bz, b_pg[11], b_c], writes=[b_pg[11]])
                    S.add("dve", lambda e, yv=yv, outc=outc, pbv=pbv: e.tensor_tensor(out=outc, in0=pbv, in1=yv, op=ALU.mult),
                          reads=[bpb, b_pg[11]], writes=b_cT)
                if pi == 0:
                    S.add("dve", lambda e, c=c: e.tensor_copy(out=sm[:, 64 + 2 * c:66 + 2 * c], in_=zc[:, 1024:1026]), reads=[b_pg[10]], writes=[b_sm])
                if pi == 1:
                    pst, bpst = nps()
                    S.add("dve", lambda e: e.tensor_copy(out=zb[:, 32:40].rearrange("p (b j) -> p b j", j=2), in_=zs[:, :, 4:6]),
                          reads=[b_sm], writes=[b_pg[11]])
                    S.add("pe", lambda e, pst=pst: e.transpose(pst[0:8, 0:128], zb[:, 32:40], ident), reads=[b_pg[11], b_c], writes=[bpst])
                    S.add("dve", lambda e, pst=pst, c=c: e.tensor_copy(out=cvst, in_=pst[0:8, 0:128]), reads=[bpst], writes=[b_acc[1]])
                    S.add("sp", lambda e, c=c: e.dma_start(out=convs[l, :, c * 128:(c + 1) * 128], in_=cvst), reads=[b_acc[1]], dma=True)
            for oc in range(8):
                wa_, bwa = load_w(wao[l], 0, 4, oc * 128, 128, 0)
                slot = (st["w"] - 1) % 3
                wc2, _ = load_w(wco[l], 0, 8, oc * 128, 128, 512, slot=slot)
                wg1, _ = load_w(wl, 0, 8, 7680 + oc * 128, 128, 1536, slot=slot)
                wg2, _ = load_w(wl, 0, 8, 8704 + oc * 128, 128, 2560, slot=slot)
                for (s0, sn) in subs:
                    ti = tis[s0 // 512]
                    t0 = p0 + s0
                    p1, bp1 = nps()
                    for k in range(4):
                        S.add("pe", lambda e, p1=p1, k=k, s0=s0, sn=sn, wa_=wa_: e.matmul(p1[:, 0:sn], lhsT=wa_[:, k, :], rhs=aT[:, k, s0:s0 + sn], start=(k == 0), stop=(k == 3)),
                              reads=[bwa] + b_aT, writes=[bp1])
                    p2, bp2 = nps()
                    for k in range(8):
                        S.add("pe", lambda e, p2=p2, k=k, s0=s0, sn=sn, wc2=wc2: e.matmul(p2[:, 0:sn], lhsT=wc2[:, k, :], rhs=cT[:, k, s0:s0 + sn], start=(k == 0), stop=(k == 7)),
                              reads=[bwa] + b_cT, writes=[bp2])
                    g1p, bg1 = nps()
                    g2p, bg2 = nps()
                    for gp, bgp, wg in ((g1p, bg1, wg1), (g2p, bg2, wg2)):
                        for k in range(8):
                            S.add("pe", lambda e, gp=gp, k=k, t0=t0, sn=sn, wg=wg: e.matmul(gp[:, 0:sn], lhsT=wg[:, k, :], rhs=uv(k, t0, sn), start=(k == 0), stop=(k == 7)),
                                  reads=[bwa, b_u[k][ti]], writes=[bgp])
                    s1, s2 = pagef(10)[:, 0:sn], pagef(11)[:, 0:sn]
                    S.add("act", lambda e, g1p=g1p, s1=s1, sn=sn, oc=oc: e.activation(out=s1, in_=g1p[:, 0:sn], func=AF.Sigmoid,
                                                                                     bias=cf[:, BG + l * 16 + oc:BG + l * 16 + oc + 1]), reads=[bg1, b_c], writes=[b_pg[10]])
                    S.add("act", lambda e, g2p=g2p, s2=s2, sn=sn, oc=oc: e.activation(out=s2, in_=g2p[:, 0:sn], func=AF.Sigmoid,
                                                                                     bias=cf[:, BG + l * 16 + 8 + oc:BG + l * 16 + 8 + oc + 1]), reads=[bg2, b_c], writes=[b_pg[11]])
                    S.add("dve", lambda e, p1=p1, s1=s1, sn=sn: e.tensor_tensor(out=s1, in0=p1[:, 0:sn], in1=s1, op=ALU.mult), reads=[bp1, b_pg[10]], writes=[b_pg[10]])
                    S.add("dve", lambda e, p2=p2, s2=s2, sn=sn: e.tensor_tensor(out=s2, in0=p2[:, 0:sn], in1=s2, op=ALU.mult), reads=[bp2, b_pg[11]], writes=[b_pg[11]])
                    S.add("dve", lambda e, s1=s1, s2=s2, oc=oc, s0=s0, sn=sn: e.tensor_tensor(out=mT[:, oc, s0:s0 + sn], in0=s1, in1=s2, op=ALU.add),
                          reads=[b_pg[10], b_pg[11]], writes=b_mT)
            for oc in range(8):
                wv, bw = load_w(wo[l], 0, 8, oc * 128, 128)
                for (s0, sn) in subs:
                    ti = tis[s0 // 512]
                    t0 = p0 + s0
                    ps, bp = nps()
                    for k in range(8):
                        S.add("pe", lambda e, ps=ps, k=k, s0=s0, sn=sn, wv=wv: e.matmul(ps[:, 0:sn], lhsT=wv[:, k, :], rhs=mT[:, k, s0:s0 + sn], start=(k == 0), stop=(k == 7)),
                              reads=[bw] + b_mT, writes=[bp])
                    S.add("dve", lambda e, ps=ps, oc=oc, t0=t0, sn=sn: e.tensor_tensor(out=xv(oc, t0, sn), in0=ps[:, 0:sn], in1=xv(oc, t0, sn), op=ALU.add),
                          reads=[bp, b_x[oc][ti]], writes=[b_x[oc][ti]])

    for l in range(2):
        if STAGE >= 2 + 10 * l:
            ffn(f1i[l], f1o[l], l * 3 + 0)
        if STAGE >= 3 + 10 * l:
            mixer(l)
        if STAGE >= 9 + 10 * l:
            ffn(f2i[l], f2o[l], l * 3 + 2)
    if STAGE >= 1:
        rmsnorm(6, final=True)
    for blk in range(NBLK):
        r0 = blk * 128
        nr = 128 if blk < 16 else 16
        ti = blk // 4 if blk < 16 else 4
        stg = pagef(2 * (blk % 2), 2)[:, 0:1024]
        bs = [b_pg[2 * (blk % 2)], b_pg[2 * (blk % 2) + 1]]
        for half in range(2):
            ps, bp = nps()
            for cc in range(4):
                c = half * 4 + cc
                S.add("pe", lambda e, ps=ps, c=c, cc=cc, r0=r0, nr=nr: e.transpose(ps[0:nr, cc * 128:(cc + 1) * 128], xv(c, r0, nr), ident),
                      reads=[b_x[c][ti], b_c], writes=[bp])
            S.add("dve" if half else "act",
                  (lambda e, ps=ps, stg=stg, nr=nr, half=half: e.tensor_copy(out=stg[0:nr, half * 512:(half + 1) * 512], in_=ps[0:nr, :])) if half else
                  (lambda e, ps=ps, stg=stg, nr=nr, half=half: e.copy(out=stg[0:nr, half * 512:(half + 1) * 512], in_=ps[0:nr, :])),
                  reads=[bp], writes=bs)
        S.add("sp", lambda e, stg=stg, r0=r0, nr=nr: e.dma_start(out=y[r0:r0 + nr, :], in_=stg[0:nr, :]), reads=bs, dma=True)

    while pending:
        S.add("sp", pending.pop(0), dma=True)
    S.emit(nc, stack)
    stack.close()
    return nc


_NC = None


def _consts():
    cst_f = np.zeros((128, 1280), np.float32)
    cst_f[:, 256:384] = np.eye(128, dtype=np.float32)
    cst_f[:, 384:512] = 1.0 / 1024.0
    for h in range(8):
        cst_f[h, 512 + h * 64:512 + (h + 1) * 64] = 1.0
    cst_f[0:8, 1024:1032] = np.eye(8, dtype=np.float32)
    cst_f[0:8, 1032] = 1.0
    MKS = 1040
    for g in range(3):
        for t in range(4):
            col = MKS + g * 4 + t
            if g == 0:
                cst_f[:, col] = (np.arange(128) >= t).astype(np.float32)
            else:
                cst_f[:, col] = 1.0
    for g in range(3):
        for bt in range(16):
            b, t = bt // 4, bt % 4
            col = MKS + 16 + g * 16 + bt
            for key in range(16):
                kb, kt = key // 4, key % 4
                ok = (kb == b) and ((kt <= t) if g == 0 else (kt == t))
                cst_f[key, col] = 1.0 if ok else 0.0
    cst_s = np.zeros((16, 2048), np.float32)
    for bt in range(16):
        cst_s[bt, bt * 128:(bt + 1) * 128] = 1.0
    k = np.arange(128)[:, None]
    q = np.arange(128)[None, :]
    mprev = (k >= q).astype(np.float32)
    mcur = (k <= q).astype(np.float32)
    return cst_f, cst_s, mprev, mcur


def kernel(x_prompt, x_sample, cache_kv1, cache_kv2, cache_kv3, state_conv,
           ffn1_norm, ffn1_w_in, ffn1_w_out, mix_norm, w_in, b_gate, conv_w,
           w_attn_out, w_conv_out, w_out, ffn2_norm, ffn2_w_in, ffn2_w_out, final_norm):
    global _NC
    if _NC is None:
        _NC = build()
    nc = _NC
    f = lambda a: np.ascontiguousarray(np.asarray(a, dtype=np.float32))
    x_prompt, x_sample = f(x_prompt), f(x_sample)
    caches = [f(cache_kv1), f(cache_kv2), f(cache_kv3)]
    state_conv = f(state_conv)
    cst_f0, cst_s, mprev, mcur = _consts()
    gains = np.stack([f(ffn1_norm)[0], f(mix_norm)[0], f(ffn2_norm)[0], f(ffn1_norm)[1], f(mix_norm)[1], f(ffn2_norm)[1], f(final_norm)])
    cst_f0[:, 0:56] = gains.reshape(7, 8, 128).transpose(2, 0, 1).reshape(128, 56)
    cst_f0[:, 56:88] = f(b_gate).reshape(2, 16, 128).transpose(2, 0, 1).reshape(128, 32)
    cst_f0[:, 88:136] = f(conv_w).reshape(2, 3, 8, 128).transpose(3, 0, 1, 2).reshape(128, 48)
    shared = {"f1i": f(ffn1_w_in), "f1o": f(ffn1_w_out), "f2i": f(ffn2_w_in), "f2o": f(ffn2_w_out), "win": f(w_in),
              "wao": f(w_attn_out), "wco": f(w_conv_out), "wo": f(w_out)}
    in_maps = []
    for c in range(8):
        sq, j = c // 4, c % 4
        xin = np.concatenate([x_prompt[sq, j * 2048:(j + 1) * 2048], x_sample[c * 4:(c + 1) * 4].reshape(16, 1024)], 0)
        cf_ = cst_f0.copy()
        cf_[:, 136] = 1.0 if j == 1 else 0.0
        cf_[:, 137] = 1.0 if j == 2 else 0.0
        cf_[:, 139] = 1.0 if j == 3 else 0.0
        cf_[:, 138] = 0.0 if j == 0 else 1.0
        cbm = np.zeros((128, 768), np.float32)
        NEGM = -30000.0
        cbm[:, 0:128], cbm[:, 128:256] = (1.0 - mprev) * NEGM, (1.0 - mcur) * NEGM
        cbm[:, 256:384] = cbm[:, 0:128] if j != 0 else NEGM
        cbm[:, 384:512] = cbm[:, 128:256]
        cbm[:, 512:640] = 1.0
        cbm[:, 640:768] = np.eye(128, dtype=np.float32)
        m = dict(shared)
        m.update({"xin": np.ascontiguousarray(xin), "cst_f": cf_, "cst_b": cbm.astype(ml_dtypes.bfloat16),
                  "sconv": np.ascontiguousarray(state_conv[:, c * 4:(c + 1) * 4].reshape(2, 8, 1024))})
        for g in range(3):
            m[f"ck{g}"] = np.ascontiguousarray(caches[g][:, c * 4:(c + 1) * 4].reshape(2, 4, LB[g], 1024))
        in_maps.append(m)
    res = run_bass_kernel_spmd(nc, in_maps, core_ids=list(range(8)))
    R = res.results
    yp = np.stack([np.concatenate([R[sq * 4 + j]["y"][0:2048] for j in range(4)], 0) for sq in range(2)])
    ys = np.concatenate([R[c]["y"][2048:2064].reshape(4, 4, 1024) for c in range(8)], 0)
    outs = [yp, ys]
    for g in range(3):
        outs.append(np.stack([R[3][f"kvp{g}"], R[7][f"kvp{g}"]], 1).reshape(2, 2, LB[g], 2, 8, 64))
    outs.append(np.stack([R[3]["convp"], R[7]["convp"]], 1))
    for g in range(3):
        outs.append(np.concatenate([R[c][f"kvs{g}"] for c in range(8)], 1).reshape(2, 32, LB[g], 2, 8, 64))
    outs.append(np.concatenate([R[c]["convs"].reshape(2, 4, 2, 1024) for c in range(8)], 1))
    return tuple(np.ascontiguousarray(o.astype(np.float32)) for o in outs)
```

```python
import numpy as np
import ml_dtypes
import concourse.bass as bass
import concourse.mybir as mybir
from concourse.bass_utils import run_bass_kernel_spmd

F32, BF16 = mybir.dt.float32, mybir.dt.bfloat16
AF = mybir.ActivationFunctionType
ALU = mybir.AluOpType
AX = mybir.AxisListType

NTOK, NP, NS = 2064, 2048, 16
PG = 2080
NPAGE = 12
TILES = [(0, 512), (512, 512), (1024, 512), (1536, 512), (2048, 16)]
DIL = [1, 4, 16]
LB = [128, 512, 2048]
HK = [128, 512, 2048]
NHB = [1, 4, 16]
KOFF = {}
_o = 0
for _g in (2, 1, 0):
    for _pr in range(4):
        KOFF[(_g, _pr)] = _o
        _o += HK[_g]
VOFF = {}
for _g in range(3):
    VOFF[_g] = _o
    _o += NHB[_g] * 512
ZOFF = _o
PW = _o + 16
CH = 4096
NCH = (PW + CH - 1) // CH
EPS = 1e-6
SAME_ENG_SYNC = True
import os
STAGE = int(os.environ.get("KSTAGE", "99"))


class Buf:
    __slots__ = ("name", "w", "rc", "rd", "excl")

    def __init__(self, name, excl=False):
        self.name = name
        self.excl = excl
        self.w = None
        self.rc = {}
        self.rd = []


class Op:
    __slots__ = ("eng", "fn", "deps", "signal", "seq", "dma", "dsem", "dval", "idx")


class Sched:
    ENGS = ("pe", "act", "dve", "pool", "sp")
    NDS = {"sp": 8, "pool": 8, "act": 2}
    SEMCAP = 12000

    def __init__(self):
        self.q = {e: [] for e in self.ENGS}

    def add(self, eng, fn, reads=(), writes=(), dma=False, append=False):
        op = Op()
        op.eng, op.fn, op.dma, op.signal, op.seq = eng, fn, dma, False, None
        writes = list(writes) + [b for b in reads if b.excl]
        reads = [b for b in reads if not b.excl]
        deps = []
        for b in reads:
            if b.w is not None:
                deps.extend(b.w)
        for b in writes:
            if b.w is not None and not append:
                deps.extend(b.w)
            deps.extend(b.rc.values())
            deps.extend(b.rd)
        dd = {}
        dl = []
        for d in deps:
            if d.dma:
                dl.append(d)
            else:
                o = dd.get(d.eng)
                if o is None or d.idx > o.idx:
                    dd[d.eng] = d
        op.deps = list(dd.values()) + list({id(d): d for d in dl}.values())
        for d in op.deps:
            d.signal = True
        op.idx = len(self.q[eng])
        for b in reads:
            if dma:
                b.rd.append(op)
            else:
                b.rc[eng] = op
        for b in writes:
            if append and b.w is not None:
                b.w = b.w + [op]
            else:
                b.w = [op]
            b.rc = {}
            b.rd = []
        self.q[eng].append(op)
        return op

    def emit(self, nc, stack):
        ncs = {}
        for e in self.ENGS:
            n = sum(1 for o in self.q[e] if (not o.dma) and o.signal)
            ncs[e] = [stack.enter_context(nc.semaphore(f"c_{e}_{i}")) for i in range(n // self.SEMCAP + 1)]
        dsem = {e: [stack.enter_context(nc.semaphore(f"d_{e}_{i}")) for i in range(k)] for e, k in self.NDS.items()}
        for e in self.ENGS:
            cnt = 0
            dc = 0
            dcount = {}
            for o in self.q[e]:
                if o.dma:
                    k = dc % self.NDS[e]
                    dc += 1
                    dcount[k] = dcount.get(k, 0) + 1
                    o.dsem = (e, k)
                    o.dval = 16 * dcount[k]
                elif o.signal:
                    o.seq = (cnt // self.SEMCAP, cnt % self.SEMCAP + 1)
                    cnt += 1
        block = stack.enter_context(nc.Block())
        q = self.q

        def run(ename, e):
            known = {}
            for o in q[ename]:
                need = {}
                for d in o.deps:
                    if d.dma:
                        key, val = ("d",) + d.dsem, d.dval
                    else:
                        if d.eng == ename and (ename == "pe" or not SAME_ENG_SYNC):
                            continue
                        key, val = ("c", d.eng), d.seq
                    if key in known and known[key] >= val:
                        continue
                    if key not in need or need[key] < val:
                        need[key] = val
                if o.dma and o.dval > 16:
                    key, val = ("d",) + o.dsem, o.dval - 16
                    if not (key in known and known[key] >= val):
                        if key not in need or need[key] < val:
                            need[key] = val
                for key, val in need.items():
                    if key[0] == "d":
                        e.wait_ge(dsem[key[1]][key[2]], val)
                    else:
                        e.wait_ge(ncs[key[1]][val[0]], val[1])
                    known[key] = val
                ins = o.fn(e)
                if o.dma:
                    ins.then_inc(dsem[o.dsem[0]][o.dsem[1]], 16)
                elif o.signal:
                    ins.then_inc(ncs[ename][o.seq[0]], 1)
            if ename in self.NDS:
                last = {}
                for o in q[ename]:
                    if o.dma:
                        last[o.dsem] = o.dval
                for (en, k), v in last.items():
                    e.wait_ge(dsem[en][k], v)

        @block.tensor
        def _(e):
            run("pe", e)

        @block.scalar
        def _(e):
            run("act", e)

        @block.vector
        def _(e):
            run("dve", e)

        @block.gpsimd
        def _(e):
            run("pool", e)

        @block.sync
        def _(e):
            run("sp", e)


def build():
    from contextlib import ExitStack
    nc = bass.Bass("TRN2", target_bir_lowering=False)
    S = Sched()
    stack = ExitStack()

    MINI = bool(os.environ.get("KMINI"))

    def din(name, shape, dt=F32):
        if MINI and name not in ("xin", "cst_f", "cst_b"):
            shape = [2, 128, 128] if len(shape) == 3 else [2, 4, 4, 128]
        return nc.dram_tensor(name, list(shape), dt, kind="ExternalInput").ap()

    def dout(name, shape, dt=F32):
        if MINI and name != "y":
            shape = [2, 8, 128]
        return nc.dram_tensor(name, list(shape), dt, kind="ExternalOutput").ap()

    def dint(name, shape, dt=BF16):
        if os.environ.get("KMINI2"):
            shape = [4, 128, 128] if len(shape) == 3 else [128, 128]
        return nc.dram_tensor(name, list(shape), dt)

    xin = din("xin", [NTOK, 1024])
    ck = [din(f"ck{g}", [2, 4, LB[g], 1024]) for g in range(3)]
    sconv = din("sconv", [2, 8, 1024])
    f1i, f1o = din("f1i", [2, 1024, 5632]), din("f1o", [2, 2816, 1024])
    f2i, f2o = din("f2i", [2, 1024, 5632]), din("f2o", [2, 2816, 1024])
    win = din("win", [2, 1024, 9728])
    wao, wco, wo = din("wao", [2, 512, 1024]), din("wco", [2, 1024, 1024]), din("wo", [2, 1024, 1024])
    cst_f = din("cst_f", [128, 1280])
    cst_b = din("cst_b", [128, 768], BF16)
    y = dout("y", [NTOK, 1024])
    kvp = [dout(f"kvp{g}", [2, LB[g], 1024]) for g in range(3)]
    convp = dout("convp", [2, 2, 1024])
    kvs = [dout(f"kvs{g}", [2, 4, LB[g], 1024]) for g in range(3)]
    convs = dout("convs", [2, 8, 1024])

    KTs = dint("KTs", [12, 128, 2048]).ap()
    Vs_d = dint("Vs_d", [48, 128, 512]).ap()
    CHW = [min(CH, PW - k * CH) for k in range(NCH)]
    pack_t = [[dint(f"pack{l}_{k}", [128, CHW[k]]).ap() for k in range(NCH)] for l in range(2)]
    gA_t = [[dint(f"gA{l}_{k}", [512, CHW[k]]).ap() for k in range(NCH)] for l in range(2)]
    b_packs = [[Buf(f"pack{l}_{k}") for k in range(NCH)] for l in range(2)]
    b_gAs = [[Buf(f"gA{l}_{k}") for k in range(NCH)] for l in range(2)]
    halo = dint("halo", [128, PW]).ap()
    aTs = dint("aTs", [4, 128, NP]).ap()
    b_KTs = [Buf(f"KTs{i}") for i in range(12)]
    b_Vs = [Buf(f"Vs{i}") for i in range(48)]
    b_pack, b_gA, b_gB, b_halo = Buf("pack"), Buf("gA"), Buf("gB"), Buf("halo")
    b_aTs = [Buf(f"aTs{i}") for i in range(4)]

    def sb(name, shape, dt):
        return stack.enter_context(nc.sbuf_tensor(name, list(shape), dt))

    xT = sb("xT", [128, 8 * NTOK], F32)
    uT = sb("uT", [128, 8 * NTOK], BF16)
    arena = sb("arena", [128, NPAGE * PG], BF16)
    acc = sb("acc", [128, 2 * NTOK], F32)
    wsl = sb("wsl", [128, 3 * 4096], BF16)
    cf = sb("cf", [128, 1280], F32)
    cb = sb("cb", [128, 768], BF16)
    asT = sb("asT", [128, 64], BF16)
    sm = sb("sm", [128, 256], F32)
    smb = sb("smb", [128, 16], BF16)
    skt = sb("skt", [128, 3 * 512], F32)
    svt = sb("svt", [128, 3 * 512], BF16)
    b_skt = [Buf(f"skt{i}") for i in range(3)]
    b_svt = [Buf(f"svt{i}") for i in range(3)]
    ztb = sb("ztb", [128, 32], BF16)
    b_ztb = Buf("ztb")
    b_x = [[Buf(f"x{c}_{t}") for t in range(5)] for c in range(8)]
    b_u = [[Buf(f"u{c}_{t}") for t in range(5)] for c in range(8)]
    b_pg = [Buf(f"pg{i}") for i in range(NPAGE)]
    b_acc = [Buf("accn"), Buf("accd")]
    b_w = [Buf(f"w{i}") for i in range(3)]
    b_c = Buf("const")
    b_asT = Buf("asT")
    b_sm = Buf("sm")
    b_smb = Buf("smb")
    b_sS, b_sE, b_sP = [Buf("sS0"), Buf("sS1")], [Buf("sE0"), Buf("sE1")], [Buf("sP0"), Buf("sP1")]
    b_sX = Buf("sX")
    psum = [stack.enter_context(nc.psum_tensor(f"ps{i}", [128, 512], F32)) for i in range(8)]
    b_ps = [Buf(f"ps{i}", excl=True) for i in range(8)]
    st = {"ps": 0, "w": 0, "nbank": 8}

    def nps():
        i = st["ps"] % st["nbank"]
        st["ps"] += 1
        return psum[i], b_ps[i]

    def xv(c, t0, tn):
        return xT[:, c * NTOK + t0: c * NTOK + t0 + tn]

    def uv(c, t0, tn, step=1):
        return uT[:, c * NTOK + t0: c * NTOK + t0 + tn * step: step] if step > 1 else uT[:, c * NTOK + t0: c * NTOK + t0 + tn]

    def page(i, n=1):
        return arena[:, i * PG:(i + n) * PG]

    def pagef(i, n=1):
        return arena[:, i * PG:(i + n) * PG].bitcast(F32)

    GN, BG, CW, SEL, IDN, ONESF = 0, 56, 88, 136, 256, 384
    BMASK, EYE8, MKS = 512, 1024, 1040
    MB_N, MB_H, ONESB, IDB = 0, 256, 512, 640

    S.add("sp", lambda e: e.dma_start(out=cf[:], in_=cst_f[:, :]), writes=[b_c], dma=True)
    S.add("sp", lambda e: e.dma_start(out=cb[:], in_=cst_b[:, :]), writes=[b_c], dma=True)

    ident = cf[:, IDN:IDN + 128]
    if os.environ.get("KTOUCH"):
        b_t = Buf("touch")
        for ii, tin in enumerate([f1i, f1o, f2i, f2o, win, wao, wco, wo]):
            S.add("sp", lambda e, tin=tin, ii=ii: e.dma_start(out=sm[0:1, 240 + ii:241 + ii], in_=tin[0, 0:1, 0:1]), writes=[b_t], dma=True)
        for g in range(3):
            S.add("sp", lambda e, g=g: e.dma_start(out=sm[0:1, 250 + g:251 + g], in_=ck[g][0, 0, 0:1, 0:1]), writes=[b_t], dma=True)
        S.add("sp", lambda e: e.dma_start(out=sm[0:1, 253:254], in_=sconv[0, 0:1, 0:1]), writes=[b_t], dma=True)

    accv = acc[:, 0:2 * NTOK]
    NBLK = int(os.environ.get("KNBLK", "17"))
    for blk in range(NBLK):
        r0 = blk * 128
        nr = 128 if blk < 16 else 16
        stg = accv[:, (blk % 2) * 1024:(blk % 2) * 1024 + 1024]
        bs = b_acc[blk % 2]
        S.add("sp", lambda e, stg=stg, r0=r0, nr=nr: e.dma_start(out=stg[0:nr, :], in_=xin[r0:r0 + nr, :]),
              writes=[bs], dma=True)
        ti = blk // 4 if blk < 16 else 4
        for half in range(2):
            ps, bp = nps()
            for cc in range(4):
                c = half * 4 + cc
                S.add("pe", lambda e, ps=ps, stg=stg, c=c, cc=cc, nr=nr: e.transpose(
                    ps[:, cc * 128:cc * 128 + nr], stg[0:nr, c * 128:(c + 1) * 128], ident[0:nr, 0:nr]),
                    reads=[bs, b_c], writes=[bp])
            for cc in range(4):
                c = half * 4 + cc
                S.add("dve" if cc % 2 else "act",
                      (lambda e, ps=ps, c=c, cc=cc, r0=r0, nr=nr: e.tensor_copy(out=xv(c, r0, nr), in_=ps[:, cc * 128:cc * 128 + nr]))
                      if cc % 2 else
                      (lambda e, ps=ps, c=c, cc=cc, r0=r0, nr=nr: e.copy(out=xv(c, r0, nr), in_=ps[:, cc * 128:cc * 128 + nr])),
                      reads=[bp], writes=[b_x[c][ti]])

    pending = []
    for l in range(2 if not os.environ.get("KNOCOPY") else 0):
        for g in range(3):
            for b in range(4):
                for r0 in range(0, LB[g] - 4, 256):
                    rn = min(256, LB[g] - 4 - r0)
                    pending.append(lambda e, l=l, g=g, b=b, r0=r0, rn=rn: e.dma_start(out=kvs[g][l, b, r0:r0 + rn, :], in_=ck[g][l, b, 4 + r0:4 + r0 + rn, :]))

    def issue_copy(dep):
        if pending:
            S.add("sp", pending.pop(0), reads=[dep], dma=True)

    def load_w(wl, kc0, nkc, c0, ncol, off=0, slot=None):
        app = slot is not None
        if slot is None:
            slot = st["w"] % 3
            st["w"] += 1
        view = wsl[:, slot * 4096 + off: slot * 4096 + off + nkc * ncol].rearrange("p (k n) -> p k n", k=nkc)
        src = wl.rearrange("(k p) n -> p k n", p=128)[:, kc0:kc0 + nkc, c0:c0 + ncol]
        S.add("pool", lambda e, view=view, src=src: e.dma_start(out=view, in_=src), writes=[b_w[slot]], dma=True, append=app)
        return view, b_w[slot]

    def rmsnorm(gi, final=False):
        for ti, (t0, tn) in enumerate(TILES):
            ps, bp = nps()
            for c in range(8):
                sq = pagef(c % 2)[:, 0:tn]
                S.add("act", lambda e, sq=sq, c=c, t0=t0, tn=tn: e.activation(out=sq, in_=xv(c, t0, tn), func=AF.Square),
                      reads=[b_x[c][ti]], writes=[b_pg[c % 2]])
                S.add("pe", lambda e, ps=ps, sq=sq, c=c, tn=tn: e.matmul(ps[:, 0:tn], lhsT=cf[:, ONESF:ONESF + 128], rhs=sq,
                                                                       start=(c == 0), stop=(c == 7)),
                      reads=[b_pg[c % 2], b_c], writes=[bp])
            rs = pagef(2)[:, 0:tn]
            S.add("dve", lambda e, ps=ps, rs=rs, tn=tn: e.tensor_scalar(out=rs, in0=ps[:, 0:tn], scalar1=EPS, scalar2=None,
                                                                       op0=ALU.add),
                  reads=[bp], writes=[b_pg[2]])
            S.add("act", lambda e, rs=rs: e.activation(out=rs, in_=rs, func=AF.Sqrt), reads=[b_pg[2]], writes=[b_pg[2]])
            S.add("dve", lambda e, rs=rs: e.reciprocal(out=rs, in_=rs), reads=[b_pg[2]], writes=[b_pg[2]])
            for c in range(8):
                if final:
                    S.add("dve", lambda e, rs=rs, c=c, t0=t0, tn=tn: e.scalar_tensor_tensor(
                        out=xv(c, t0, tn), in0=xv(c, t0, tn), scalar=cf[:, GN + gi * 8 + c:GN + gi * 8 + c + 1], in1=rs,
                        op0=ALU.mult, op1=ALU.mult), reads=[b_x[c][ti], b_pg[2], b_c], writes=[b_x[c][ti]])
                else:
                    S.add("dve", lambda e, rs=rs, c=c, t0=t0, tn=tn: e.scalar_tensor_tensor(
                        out=uv(c, t0, tn), in0=xv(c, t0, tn), scalar=cf[:, GN + gi * 8 + c:GN + gi * 8 + c + 1], in1=rs,
                        op0=ALU.mult, op1=ALU.mult), reads=[b_x[c][ti], b_pg[2], b_c], writes=[b_u[c][ti]])

    def ffn(wi, wo_, gi):
        rmsnorm(gi)
        for half in range(2):
            for jj in range(11):
                j = half * 11 + jj
                if jj % 2 == 0:
                    ncol = 256 if jj < 10 else 128
                    wv, bw = load_w(wi, 0, 8, j * 128, ncol, 0)
                    slot = (st["w"] - 1) % 3
                    wv2, _ = load_w(wi, 0, 8, 2816 + j * 128, ncol, 2048, slot=slot)
                co = (jj % 2) * 128
                for ti, (t0, tn) in enumerate(TILES):
                    pg_, bg_ = nps()
                    pu_, bu_ = nps()
                    for k in range(8):
                        S.add("pe", lambda e, p=pg_, wv=wv, k=k, co=co, t0=t0, tn=tn: e.matmul(
                            p[:, 0:tn], lhsT=wv[:, k, co:co + 128], rhs=uv(k, t0, tn), start=(k == 0), stop=(k == 7)),
                            reads=[bw, b_u[k][ti]], writes=[bg_])
                    for k in range(8):
                        S.add("pe", lambda e, p=pu_, wv=wv2, k=k, co=co, t0=t0, tn=tn: e.matmul(
                            p[:, 0:tn], lhsT=wv[:, k, co:co + 128], rhs=uv(k, t0, tn), start=(k == 0), stop=(k == 7)),
                            reads=[bw, b_u[k][ti]], writes=[bu_])
                    hv = page(jj)[:, t0:t0 + tn]
                    sg = pagef(11)[:, 0:tn]
                    S.add("act", lambda e, p=pg_, sg=sg, tn=tn: e.activation(out=sg, in_=p[:, 0:tn], func=AF.Silu),
                          reads=[bg_], writes=[b_pg[11]])
                    S.add("dve", lambda e, p=pu_, sg=sg, hv=hv, tn=tn: e.tensor_tensor(out=hv, in0=p[:, 0:tn], in1=sg, op=ALU.mult),
                          reads=[bu_, b_pg[11]], writes=[b_pg[jj]])
                issue_copy(b_pg[jj])
            for oc in range(8):
                wv, bw = load_w(wo_, half * 11, 11, oc * 128, 128)
                for ti, (t0, tn) in enumerate(TILES):
                    ps, bp = nps()
                    for jj in range(11):
                        S.add("pe", lambda e, p=ps, wv=wv, jj=jj, t0=t0, tn=tn: e.matmul(
                            p[:, 0:tn], lhsT=wv[:, jj, :], rhs=page(jj)[:, t0:t0 + tn], start=(jj == 0), stop=(jj == 10)),
                            reads=[bw, b_pg[jj]], writes=[bp])
                    S.add("dve", lambda e, p=ps, oc=oc, t0=t0, tn=tn: e.scalar_tensor_tensor(
                        out=xv(oc, t0, tn), in0=p[:, 0:tn], scalar=0.5, in1=xv(oc, t0, tn), op0=ALU.mult, op1=ALU.add),
                        reads=[bp, b_x[oc][ti]], writes=[b_x[oc][ti]])

    def proj_fm(wv, bw, m0, M, ti, ps, bp):
        t0, tn = TILES[ti]
        for k in range(8):
            S.add("pe", lambda e, k=k: e.matmul(ps[0:M, 0:tn], lhsT=wv[:, k, m0:m0 + M], rhs=uv(k, t0, tn),
                                               start=(k == 0), stop=(k == 7)),
                  reads=[bw, b_u[k][ti]], writes=[bp])

    def deint(apv, d, t0):
        return apv[:, 0:NP].rearrange("p (r m) -> p r m", r=d)[:, :, t0 // d:(t0 + 512) // d]

    def mixer(l):
        wl = win[l]
        pack, gA = pack_t[l], gA_t[l]

        def pk(off, n):
            k = off // CH
            assert (off + n - 1) // CH == k
            return pack[k][:, off - k * CH: off - k * CH + n], b_packs[l][k]
        rmsnorm(l * 3 + 1)
        Qs, Ks, Vsm = page(0), pagef(2, 2), pagef(4, 2)
        for g in range(3):
            d = DIL[g]
            nb = 16 // d
            wq, bwq = load_w(wl, 0, 8, g * 512, 512)
            ps, bp = nps()
            for k in range(8):
                S.add("pe", lambda e, ps=ps, wq=wq, k=k: e.matmul(ps[0:16, :], lhsT=uv(k, NP, 16), rhs=wq[:, k, :],
                                                               start=(k == 0), stop=(k == 7)), reads=[bwq, b_u[k][4]], writes=[bp])
            S.add("act", lambda e, ps=ps, g=g: e.activation(out=Qs[0:16, g * 512:(g + 1) * 512], in_=ps[0:16, :], func=AF.Copy, scale=0.125),
                  reads=[bp], writes=[b_pg[0]])
            wk, bwk = load_w(wl, 0, 8, 1536 + g * 512, 512)
            wv_, bwv = load_w(wl, 0, 8, 3072 + g * 512, 512)
            for (wx, bwx, dst, bd) in ((wk, bwk, Ks, (b_pg[2], b_pg[3])), (wv_, bwv, Vsm, (b_pg[4], b_pg[5]))):
                ps, bp = nps()
                for k in range(8):
                    S.add("pe", lambda e, ps=ps, wx=wx, k=k: e.matmul(ps[0:16, :], lhsT=uv(k, NP, 16), rhs=wx[:, k, :],
                                                                   start=(k == 0), stop=(k == 7)), reads=[bwx, b_u[k][4]], writes=[bp])
                S.add("dve", lambda e, ps=ps, dst=dst, g=g: e.tensor_copy(out=dst[0:16, g * 512:(g + 1) * 512], in_=ps[0:16, :]),
                      reads=[bp], writes=list(bd))
            for kv, src in ((0, Ks), (1, Vsm)):
                for b in range(4):
                    S.add("sp", lambda e, g=g, kv=kv, src=src, b=b: e.dma_start(
                        out=kvs[g][l, b, LB[g] - 4:LB[g], kv * 512:(kv + 1) * 512], in_=src[b * 4:b * 4 + 4, g * 512:(g + 1) * 512]),
                        reads=[b_pg[2 + 2 * kv], b_pg[3 + 2 * kv]], dma=True)
            for r in range(d):
                for n in range(nb):
                    bi = r * nb + n
                    tail = (n == nb - 1)
                    tstart = r + d * 128 * n
                    for isk in ((0, 1) if tail else (0,)):
                        wx, bwx = (wk, bwk) if isk else (wv_, bwv)
                        ps, bp = nps()
                        for k in range(8):
                            S.add("pe", lambda e, ps=ps, wx=wx, k=k, tstart=tstart, d=d: e.matmul(
                                ps[:, :], lhsT=uv(k, tstart, 128, d), rhs=wx[:, k, :], start=(k == 0), stop=(k == 7)),
                                reads=[bwx] + [b_u[k][t] for t in range(4)], writes=[bp])
                        if not isk:
                            vst = page(6 + bi % 2)[:, 0:512]
                            S.add("act", lambda e, ps=ps, vst=vst: e.copy(out=vst, in_=ps[:, :]), reads=[bp], writes=[b_pg[6 + bi % 2]])
                            S.add("sp", lambda e, vst=vst, g=g, bi=bi: e.dma_start(out=Vs_d[g * 16 + bi], in_=vst),
                                  reads=[b_pg[6 + bi % 2]], writes=[b_Vs[g * 16 + bi]], dma=True)
                            if tail:
                                pkv, bpk = pk(VOFF[g] + r * 512, 512)
                                S.add("sp", lambda e, vst=vst, pkv=pkv: e.dma_start(out=pkv, in_=vst),
                                      reads=[b_pg[6 + bi % 2]], writes=[bpk], dma=True, append=True)
                        if tail:
                            fst = pagef(8 + (bi + isk) % 2)[:, 0:512]
                            bfs = b_pg[8 + (bi + isk) % 2]
                            S.add("dve", lambda e, ps=ps, fst=fst: e.tensor_copy(out=fst, in_=ps[:, :]), reads=[bp], writes=[bfs])
                            S.add("sp", lambda e, fst=fst, g=g, r=r, d=d, isk=isk: e.dma_start(
                                out=kvp[g][l, r:LB[g]:d, (1 - isk) * 512:(2 - isk) * 512], in_=fst), reads=[bfs], dma=True)
        if STAGE < 4 + 10 * l:
            return
        if STAGE < 5 + 10 * l:
            return
        zt = sm[:, 0:16]
        zc_todo = list(range(8))

        def zc_chunk(c):
            wv, bw = load_w(wl, 0, 8, 5632 + c * 128, 128, 0)
            slot = (st["w"] - 1) % 3
            wv2, _ = load_w(wl, 0, 8, 6656 + c * 128, 128, 1024, slot=slot)
            ps, bp = nps()
            for k in range(8):
                S.add("pe", lambda e, ps=ps, wv=wv, k=k: e.matmul(ps[:, 0:2], lhsT=wv[:, k, :], rhs=uv(k, NP - 2, 2), start=(k == 0), stop=(k == 7)),
                      reads=[bw, b_u[k][3]], writes=[bp])
            for k in range(8):
                S.add("pe", lambda e, ps=ps, wv=wv2, k=k: e.matmul(ps[:, 8:10], lhsT=wv[:, k, :], rhs=uv(k, NP - 2, 2), start=(k == 0), stop=(k == 7)),
                      reads=[bw, b_u[k][3]], writes=[bp])
            S.add("act", lambda e, ps=ps: e.copy(out=sm[:, 16:18], in_=ps[:, 8:10]), reads=[bp], writes=[b_sm])
            S.add("dve", lambda e, ps=ps, c=c: e.tensor_tensor(out=zt[:, c * 2:c * 2 + 2], in0=ps[:, 0:2], in1=sm[:, 16:18], op=ALU.mult),
                  reads=[bp, b_sm], writes=[b_sm])

        for g in range(3):
            d = DIL[g]
            for pr in range(4):
                if zc_todo:
                    zc_chunk(zc_todo.pop(0))
                wv, bw = load_w(wl, 0, 8, 1536 + g * 512 + pr * 128, 128)
                stg = page(10 + pr % 2)
                bst = b_pg[10 + pr % 2]
                for ti in range(4):
                    ps, bp = nps()
                    proj_fm(wv, bw, 0, 128, ti, ps, bp)
                    t0 = TILES[ti][0]
                    S.add("dve", lambda e, ps=ps, stg=stg, d=d, t0=t0: e.tensor_copy(
                        out=deint(stg, d, t0), in_=ps[:, :].rearrange("p (j r) -> p r j", r=d)), reads=[bp], writes=[bst])
                S.add("sp", lambda e, stg=stg, g=g, pr=pr: e.dma_start(out=KTs[g * 4 + pr], in_=stg[:, 0:NP]),
                      reads=[bst], writes=[b_KTs[g * 4 + pr]], dma=True)
                nb = 16 // d
                src = stg[:, 0:NP].rearrange("p (r n i) -> p r n i", r=d, n=nb)[:, :, nb - 1, :]
                pkv, bpk = pk(KOFF[(g, pr)], HK[g])
                dst = pkv.rearrange("p (r i) -> p r i", r=d)
                S.add("sp", lambda e, src=src, dst=dst: e.dma_start(out=dst, in_=src), reads=[bst], writes=[bpk], dma=True, append=True)
        while zc_todo:
            zc_chunk(zc_todo.pop(0))
        S.add("dve", lambda e: e.tensor_copy(out=ztb[:, 0:16], in_=zt), reads=[b_sm], writes=[b_ztb])
        pkz, bpkz = pk(ZOFF, 16)
        S.add("sp", lambda e: e.dma_start(out=pkz, in_=ztb[:, 0:16]), reads=[b_ztb], writes=[bpkz], dma=True, append=True)
        ps, bp = nps()
        S.add("pe", lambda e, ps=ps: e.transpose(ps[0:16, 0:128], zt, ident), reads=[b_sm, b_c], writes=[bp])
        S.add("dve", lambda e, ps=ps: e.tensor_copy(out=sm[0:16, 32:160], in_=ps[0:16, 0:128]), reads=[bp], writes=[b_sm])
        for c in range(8):
            S.add("sp", lambda e, c=c: e.dma_start(out=convp[l, :, c * 128:(c + 1) * 128], in_=sm[2 * c:2 * c + 2, 32:160]), reads=[b_sm], dma=True)
        if STAGE < 6 + 10 * l:
            return
        for k in range(NCH):
            S.add("pool", lambda e, k=k: e.collective_compute("AllGather", ALU.bypass, replica_groups=[[0, 1, 2, 3], [4, 5, 6, 7]],
                                                           ins=[pack[k].opt()], outs=[gA[k].opt()]), reads=[b_packs[l][k]], writes=[b_gAs[l][k]])
        sample_attention(l, Qs, Ks, Vsm)
        c0 = 0
        i = 0
        while c0 < PW:
            cn = min(2048, PW - c0)
            k = c0 // CH
            ck0 = c0 - k * CH
            pgs = [6, 7, 8] if i % 2 == 0 else [9, 10, 11]
            pv = [page(p)[:, 0:cn] for p in pgs]
            bv = [b_pg[p] for p in pgs]
            for j in range(3):
                S.add("sp", lambda e, j=j, pv=pv, k=k, ck0=ck0, cn=cn: e.dma_start(out=pv[j], in_=gA[k][j * 128:(j + 1) * 128, ck0:ck0 + cn]),
                      reads=[b_gAs[l][k]], writes=[bv[j]], dma=True)
            S.add("dve", lambda e, pv=pv: e.tensor_scalar(out=pv[0], in0=pv[0], scalar1=cf[:, SEL:SEL + 1], scalar2=None, op0=ALU.mult),
                  reads=[bv[0], b_c], writes=[bv[0]])
            S.add("dve", lambda e, pv=pv: e.scalar_tensor_tensor(out=pv[0], in0=pv[1], scalar=cf[:, SEL + 1:SEL + 2], in1=pv[0],
                                                                op0=ALU.mult, op1=ALU.add), reads=[bv[0], bv[1], b_c], writes=[bv[0]])
            S.add("dve", lambda e, pv=pv: e.scalar_tensor_tensor(out=pv[0], in0=pv[2], scalar=cf[:, SEL + 3:SEL + 4], in1=pv[0],
                                                                op0=ALU.mult, op1=ALU.add), reads=[bv[0], bv[2], b_c], writes=[bv[0]])
            S.add("sp", lambda e, pv=pv, c0=c0, cn=cn: e.dma_start(out=halo[:, c0:c0 + cn], in_=pv[0]), reads=[bv[0]], writes=[b_halo], dma=True, append=True)
            c0 += cn
            i += 1
        if STAGE < 7 + 10 * l:
            return
        for pr in range(4):
            for g in range(3):
                attention(l, pr, g, wl)
            stg = page(10 + pr % 2)
            bst = b_pg[10 + pr % 2]
            S.add("dve", lambda e: e.reciprocal(out=acc[:, NTOK:NTOK + NP], in_=acc[:, NTOK:NTOK + NP]), reads=[b_acc[1]], writes=[b_acc[1]])
            S.add("dve", lambda e, stg=stg: e.tensor_tensor(out=stg[:, 0:NP], in0=acc[:, 0:NP], in1=acc[:, NTOK:NTOK + NP], op=ALU.mult),
                  reads=b_acc, writes=[bst])
            S.add("sp", lambda e, stg=stg, pr=pr: e.dma_start(out=aTs[pr], in_=stg[:, 0:NP]), reads=[bst], writes=[b_aTs[pr]], dma=True)
        if STAGE < 8 + 10 * l:
            return
        out_phase(l)

    def attention(l, pr, g, wl):
        d = DIL[g]
        nb = 16 // d
        s = (pr * 3 + g) % 2
        P0 = 5 * s
        qt, kto, kth, vo, vh = page(P0), page(P0 + 1), page(P0 + 2), page(P0 + 3), page(P0 + 4)
        bq, bko, bkh, bvo, bvh = (b_pg[P0 + i] for i in range(5))
        S.add("sp", lambda e: e.dma_start(out=kto[:, 0:NP], in_=KTs[g * 4 + pr]), reads=[b_KTs[g * 4 + pr]], writes=[bko], dma=True)
        S.add("sp", lambda e: e.dma_start(out=kth[:, 0:HK[g]], in_=halo[:, KOFF[(g, pr)]:KOFF[(g, pr)] + HK[g]]),
              reads=[b_halo], writes=[bkh], dma=True)
        S.add("sp", lambda e: e.dma_start(out=vo[:, 0:NP].rearrange("p (b c) -> p b c", c=128),
                                          in_=Vs_d[g * 16:(g + 1) * 16, :, pr * 128:(pr + 1) * 128].rearrange("b p c -> p b c")),
              reads=b_Vs[g * 16:(g + 1) * 16], writes=[bvo], dma=True)
        S.add("sp", lambda e: e.dma_start(out=vh[:, 0:NHB[g] * 128].rearrange("p (b c) -> p b c", c=128),
                                          in_=halo[:, VOFF[g]:VOFF[g] + NHB[g] * 512].rearrange("p (b c) -> p b c", c=512)[:, :, pr * 128:(pr + 1) * 128]),
              reads=[b_halo], writes=[bvh], dma=True)
        wv, bw = load_w(wl, 0, 8, g * 512 + pr * 128, 128)
        for ti in range(4):
            ps, bp = nps()
            proj_fm(wv, bw, 0, 128, ti, ps, bp)
            t0 = TILES[ti][0]
            S.add("act", lambda e, ps=ps, t0=t0: e.activation(out=deint(qt, d, t0), in_=ps[:, :].rearrange("p (j r) -> p r j", r=d),
                                                             func=AF.Copy, scale=0.125), reads=[bp], writes=[bq])
        ones = cb[:, ONESB:ONESB + 128]
        for hh in range(2):
            hp = slice(64 * hh, 64 * hh + 64)
            for qd in range(4):
                pn, bpn = nps()
                pd, bpd = nps()
                pts = []
                for half in range(2):
                    ps, bp = nps()
                    for uu in range(2):
                        bi = qd * 4 + half * 2 + uu
                        r, n = bi // nb, bi % nb
                        qv = qt[hp, bi * 128:(bi + 1) * 128]
                        if n > 0:
                            kp, bkp = kto[hp, (bi - 1) * 128:bi * 128], bko
                        else:
                            kp, bkp = kth[hp, r * 128:(r + 1) * 128], bkh
                        kc_ = kto[hp, bi * 128:(bi + 1) * 128]
                        mo = MB_H if n == 0 else MB_N
                        S.add("pe", lambda e, ps=ps, uu=uu, mo=mo: e.matmul(ps[:, uu * 256:uu * 256 + 256], lhsT=cb[:, IDB:IDB + 128], rhs=cb[:, mo:mo + 256],
                                                                           start=True, stop=False, skip_group_check=True), reads=[b_c], writes=[bp])
                        S.add("pe", lambda e, ps=ps, kp=kp, qv=qv, uu=uu: e.matmul(ps[:, uu * 256:uu * 256 + 128], lhsT=kp, rhs=qv, start=False, stop=False,
                                                                                  skip_group_check=True), reads=[bkp, bq], writes=[bp])
                        S.add("pe", lambda e, ps=ps, kc_=kc_, qv=qv, uu=uu: e.matmul(ps[:, uu * 256 + 128:uu * 256 + 256], lhsT=kc_, rhs=qv, start=False, stop=True,
                                                                                    skip_group_check=True), reads=[bko, bq], writes=[bp])
                    pt = page(10 + half)[:, 0:512]
                    bpt = b_pg[10 + half]
                    S.add("act", lambda e, ps=ps, pt=pt: e.activation(out=pt, in_=ps[:, :], func=AF.Exp), reads=[bp], writes=[bpt])
                    pts.append((pt, bpt))
                for half in range(2):
                    pt, bpt = pts[half]
                    for uu in range(2):
                        bi = qd * 4 + half * 2 + uu
                        r, n = bi // nb, bi % nb
                        u4 = half * 2 + uu
                        if n > 0:
                            vp, bvp = vo[:, (bi - 1) * 128:bi * 128], bvo
                        else:
                            vp, bvp = vh[:, r * 128:(r + 1) * 128], bvh
                        vc = vo[:, bi * 128:(bi + 1) * 128]
                        o_n = pn[:, u4 * 128:(u4 + 1) * 128]
                        o_d = pd[:, u4 * 128:(u4 + 1) * 128]
                        S.add("pe", lambda e, o_n=o_n, vp=vp, pt=pt, uu=uu: e.matmul(o_n, lhsT=vp, rhs=pt[:, uu * 256:uu * 256 + 128], start=True, stop=False),
                              reads=[bvp, bpt], writes=[bpn])
                        S.add("pe", lambda e, o_n=o_n, vc=vc, pt=pt, uu=uu: e.matmul(o_n, lhsT=vc, rhs=pt[:, uu * 256 + 128:uu * 256 + 256], start=False, stop=True),
                              reads=[bvo, bpt], writes=[bpn])
                        S.add("pe", lambda e, o_d=o_d, pt=pt, uu=uu: e.matmul(o_d, lhsT=ones, rhs=pt[:, uu * 256:uu * 256 + 128], start=True, stop=False),
                              reads=[b_c, bpt], writes=[bpd])
                        S.add("pe", lambda e, o_d=o_d, pt=pt, uu=uu: e.matmul(o_d, lhsT=ones, rhs=pt[:, uu * 256 + 128:uu * 256 + 256], start=False, stop=True),
                              reads=[b_c, bpt], writes=[bpd])
                for which, (pp, bpp) in enumerate(((pn, bpn), (pd, bpd))):
                    base = which * NTOK
                    if g == 0:
                        dst = acc[hp, base + qd * 512: base + qd * 512 + 512]
                        src = pp[hp, :]
                    elif g == 1:
                        dst = acc[hp, base + qd: base + NP: 4]
                        src = pp[hp, :]
                    else:
                        dst = acc[hp, base: base + NP].rearrange("p (m r) -> p m r", r=16)[:, :, qd * 4:qd * 4 + 4]
                        src = pp[hp, :].rearrange("p (r m) -> p m r", r=4)
                    if g == 0:
                        S.add("act", lambda e, dst=dst, src=src: e.copy(out=dst, in_=src), reads=[bpp], writes=[b_acc[which]])
                    else:
                        S.add("dve", lambda e, dst=dst, src=src: e.tensor_tensor(out=dst, in0=dst, in1=src, op=ALU.add),
                              reads=[bpp, b_acc[which]], writes=[b_acc[which]])

    def sample_attention(l, Qs, Ks, Vsm):
        st["nbank"] = 6
        prods = [pagef(8)[:, 0:512], pagef(11)[:, 0:512]]
        b_prod = [b_pg[8], b_pg[11]]
        vn_b = page(9)[:, 0:1536]
        S.add("dve", lambda e: e.tensor_copy(out=vn_b[0:16, :], in_=Vsm[0:16, 0:1536]), reads=[b_pg[4], b_pg[5]], writes=[b_pg[9]])
        pa_, bpa = psum[7], b_ps[7]
        po, bpo = psum[6], b_ps[6]
        pdn, bpdn = psum[7][:, 64:128], b_ps[7]
        steps = [(bt, g, ks) for bt in range(16) for g in range(3) for ks in range(2)]
        info = {}

        def prep(i):
            bt, g, ks = steps[i]
            b, t = bt // 4, bt % 4
            d = DIL[g]
            NK = 128 if ks == 0 else 16
            if ks == 0:
                si = (i // 2) % 3
                kc_t = skt[:, si * 512:(si + 1) * 512]
                vc_b = svt[:, si * 512:(si + 1) * 512]
                S.add("sp", lambda e: e.dma_start(out=kc_t, in_=ck[g][l, b, (t if d > 1 else 0):LB[g]:d, 0:512]), writes=[b_skt[si]], dma=True)
                S.add("pool", lambda e: e.dma_start(out=vc_b, in_=ck[g][l, b, (t if d > 1 else 0):LB[g]:d, 512:1024]), writes=[b_svt[si]], dma=True)
                kt_, bk_, vt_, bv_ = kc_t, [b_skt[si]], vc_b, [b_svt[si]]
                mcol = cf[:, MKS + g * 4 + t:MKS + g * 4 + t + 1]
            else:
                kt_, bk_, vt_, bv_ = Ks[0:16, g * 512:(g + 1) * 512], [b_pg[2], b_pg[3]], vn_b[0:16, g * 512:(g + 1) * 512], [b_pg[9]]
                mcol = cf[0:16, MKS + 16 + (g * 16 + bt):MKS + 16 + (g * 16 + bt) + 1]
            pq, bpq = nps()
            S.add("pe", lambda e: e.matmul(pq[0:NK, :], lhsT=cb[0:16, IDB + bt:IDB + bt + 1].to_broadcast([16, NK]),
                                           rhs=Qs[0:16, g * 512:(g + 1) * 512], start=True, stop=True), reads=[b_c, b_pg[0]], writes=[bpq])
            info[i] = (NK, kt_, bk_, vt_, bv_, mcol, pq, bpq)

        prep(0)
        for i, (bt, g, ks) in enumerate(steps):
            NK, kt_, bk_, vt_, bv_, mcol, pq, bpq = info.pop(i)
            par = i % 2
            prod, bprod = prods[par], b_prod[par]
            sS, sE, sP = sm[:, 192 + 8 * par:200 + 8 * par], sm[:, 208 + 8 * par:216 + 8 * par], smb[:, 8 * par:8 * par + 8]
            bS, bE, bP = b_sS[par], b_sE[par], b_sP[par]
            step = i % 6
            S.add("dve", lambda e, pq=pq, NK=NK, kt_=kt_, prod=prod: e.tensor_tensor(out=prod[0:NK, :], in0=kt_[0:NK, :], in1=pq[0:NK, :], op=ALU.mult),
                  reads=[bpq] + bk_, writes=[bprod])
            S.add("dve", lambda e, NK=NK, prod=prod, sS=sS: e.tensor_reduce(out=sS[0:NK, :], in_=prod[0:NK, :].rearrange("p (h e) -> p h e", e=64),
                                                                         axis=AX.X, op=ALU.add), reads=[bprod], writes=[bS])
            S.add("act", lambda e, NK=NK, sS=sS, sE=sE: e.activation(out=sE[0:NK, :], in_=sS[0:NK, :], func=AF.Exp), reads=[bS], writes=[bE])
            S.add("dve", lambda e, NK=NK, mcol=mcol, sE=sE, sP=sP: e.tensor_scalar(out=sP[0:NK, :], in0=sE[0:NK, :], scalar1=mcol[0:NK, :], scalar2=None,
                                                                                 op0=ALU.mult), reads=[bE, b_c], writes=[bP])
            if i + 1 < len(steps):
                prep(i + 1)
            S.add("pe", lambda e, NK=NK, vt_=vt_, step=step, sP=sP: e.matmul(po[0:8, 0:512], lhsT=sP[0:NK, :], rhs=vt_[0:NK, :],
                                                                           start=(step == 0), stop=(step == 5)), reads=[bP] + bv_, writes=[bpo])
            S.add("pe", lambda e, NK=NK, step=step, sP=sP: e.matmul(pdn[0:8, 0:8], lhsT=sP[0:NK, :], rhs=cb[0:NK, ONESB:ONESB + 8],
                                                                  start=(step == 0), stop=(step == 5)), reads=[bP, b_c], writes=[bpdn])
            if step != 5:
                continue
            ex = pagef(10)[:, 0:512]
            S.add("dve", lambda e: e.tensor_tensor(out=ex[0:8, 0:512], in0=po[0:8, 0:512], in1=cf[0:8, BMASK:BMASK + 512], op=ALU.mult),
                  reads=[bpo, b_c], writes=[b_pg[10]])
            S.add("dve", lambda e: e.tensor_tensor(out=sm[0:8, 224:232], in0=pdn[0:8, 0:8], in1=cf[0:8, EYE8:EYE8 + 8], op=ALU.mult),
                  reads=[bpdn, b_c], writes=[b_sX])
            pe1, bpe1 = nps()
            S.add("pe", lambda e, pe1=pe1: e.matmul(pe1[0:1, 0:512], lhsT=cf[0:8, EYE8 + 8:EYE8 + 9], rhs=ex[0:8, 0:512], start=True, stop=True),
                  reads=[b_pg[10], b_c], writes=[bpe1])
            pe2, bpe2 = nps()
            S.add("pe", lambda e, pe2=pe2: e.matmul(pe2[0:1, 0:8], lhsT=cf[0:8, EYE8 + 8:EYE8 + 9], rhs=sm[0:8, 224:232], start=True, stop=True),
                  reads=[b_sX, b_c], writes=[bpe2])
            S.add("dve", lambda e, pe2=pe2: e.reciprocal(out=sm[0:1, 232:240], in_=pe2[0:1, 0:8]), reads=[bpe2], writes=[b_sX])
            arow = pagef(10)[0:1, 520:1032]
            S.add("dve", lambda e, pe1=pe1, arow=arow: e.tensor_tensor(out=arow.rearrange("p (h e) -> p h e", e=64), in0=pe1[0:1, 0:512].rearrange("p (h e) -> p h e", e=64),
                                                                      in1=sm[0:1, 232:240].rearrange("p (h o) -> p h o", o=1).to_broadcast([1, 8, 64]), op=ALU.mult),
                  reads=[bpe1, b_sX], writes=[b_pg[10]])
            for pr in range(4):
                S.add("pe", lambda e, pr=pr, bt=bt, arow=arow: e.matmul(pa_[:, pr * 16 + bt:pr * 16 + bt + 1], lhsT=arow[0:1, pr * 128:(pr + 1) * 128],
                                                                       rhs=cf[0:1, EYE8:EYE8 + 1], start=True, stop=True), reads=[b_pg[10], b_c], writes=[bpa])
        S.add("dve", lambda e: e.tensor_copy(out=asT[:, 0:64], in_=pa_[:, 0:64]), reads=[bpa], writes=[b_asT])
        st["nbank"] = 8

    def out_phase(l):
        wl = win[l]
        parts = [(0, 1024), (1024, 1040)]
        for pi, (p0, pn) in enumerate(parts):
            out_part(l, wl, pi, p0, pn)

    def out_part(l, wl, pi, p0, pn):
        if True:
            subs = [(0, 512), (512, 512)] if pi == 0 else [(0, 512), (512, 512), (1024, 16)]
            tis = {0: [0, 1], 1: [2, 3, 4]}[pi]
            aT = arena[:, 8 * PG:8 * PG + 4 * pn].rearrange("p (c n) -> p c n", c=4)
            cT = arena[:, 0:8 * pn].rearrange("p (c n) -> p c n", c=8)
            mT = arena[:, 4 * PG:4 * PG + 8 * pn].rearrange("p (c n) -> p c n", c=8)
            b_aT, b_cT, b_mT = [b_pg[8], b_pg[9]], [b_pg[0], b_pg[1], b_pg[2], b_pg[3]], [b_pg[4], b_pg[5], b_pg[6], b_pg[7]]
            for pr in range(4):
                S.add("sp", lambda e, pr=pr, aT=aT, p0=p0: e.dma_start(out=aT[:, pr, 0:1024], in_=aTs[pr, :, p0:p0 + 1024]),
                      reads=[b_aTs[pr]], writes=b_aT, dma=True, append=(pr > 0))
            if pi == 1:
                S.add("dve", lambda e, aT=aT: e.tensor_copy(out=aT[:, :, 1024:1040], in_=asT[:, 0:64].rearrange("p (c n) -> p c n", c=4)),
                      reads=[b_asT], writes=b_aT, append=True)
            zs = sm[:, 160:184].rearrange("p (b j) -> p b j", j=6)
            scst = acc[0:8, 0:1024]
            cvst = acc[0:8, NTOK:NTOK + 128]
            zc = pagef(10)
            zb = pagef(11)
            for c in range(8):
                wb_, bwb = load_w(wl, 0, 8, 4608 + c * 128, 128, 0)
                slot = (st["w"] - 1) % 3
                wc_, _ = load_w(wl, 0, 8, 5632 + c * 128, 128, 1024, slot=slot)
                wh_, _ = load_w(wl, 0, 8, 6656 + c * 128, 128, 2048, slot=slot)
                w0, w1, w2 = (cf[:, CW + (l * 3 + j) * 8 + c:CW + (l * 3 + j) * 8 + c + 1] for j in range(3))
                if pi == 0:
                    if c == 0:
                        S.add("sp", lambda e: e.dma_start(out=ztb[:, 16:32], in_=halo[:, ZOFF:ZOFF + 16]), reads=[b_halo], writes=[b_ztb], dma=True)
                    S.add("dve", lambda e, c=c: e.tensor_scalar(out=zc[:, 0:2], in0=ztb[:, 16 + 2 * c:18 + 2 * c], scalar1=cf[:, SEL + 2:SEL + 3], scalar2=None,
                                                               op0=ALU.mult), reads=[b_ztb, b_c], writes=[b_pg[10]])
                elif pi == 1:
                    S.add("dve", lambda e, c=c: e.tensor_copy(out=zc[:, 0:2], in_=sm[:, 64 + 2 * c:66 + 2 * c]), reads=[b_sm], writes=[b_pg[10]])
                for (s0, sn) in subs:
                    ti = tis[s0 // 512]
                    t0 = p0 + s0
                    pcs = []
                    for wv in (wb_, wc_, wh_):
                        ps, bp = nps()
                        for k in range(8):
                            S.add("pe", lambda e, ps=ps, wv=wv, k=k, t0=t0, sn=sn: e.matmul(ps[:, 0:sn], lhsT=wv[:, k, :], rhs=uv(k, t0, sn), start=(k == 0), stop=(k == 7)),
                                  reads=[bwb, b_u[k][ti]], writes=[bp])
                        pcs.append((ps, bp))
                    (pb_, bpb), (pc_, bpc), (ph_, bph) = pcs
                    S.add("act", lambda e, ph_=ph_, sn=sn: e.copy(out=zb[:, 0:sn], in_=ph_[:, 0:sn]), reads=[bph], writes=[b_pg[11]])
                    is_s = (t0 == NP)
                    bz = b_sm if is_s else b_pg[10]
                    if not is_s:
                        S.add("dve", lambda e, pc_=pc_, s0=s0, sn=sn: e.tensor_tensor(out=zc[:, 2 + s0:2 + s0 + sn], in0=pc_[:, 0:sn], in1=zb[:, 0:sn], op=ALU.mult),
                              reads=[bpc, b_pg[11]], writes=[b_pg[10]])
                    else:
                        S.add("dve", lambda e, pc_=pc_: e.tensor_tensor(out=zs[:, :, 2:6],
                                                                       in0=pc_[:, 0:16].rearrange("p (b j) -> p b j", j=4),
                                                                       in1=zb[:, 0:16].rearrange("p (b j) -> p b j", j=4), op=ALU.mult),
                              reads=[bpc, b_pg[11]], writes=[b_sm])
                        if c == 0:
                            S.add("sp", lambda e: e.dma_start(out=scst, in_=sconv[l, :, :]), writes=[b_acc[0]], dma=True)
                        srcst = scst[:, c * 128:(c + 1) * 128]
                        pst, bpst = nps()
                        S.add("pe", lambda e, pst=pst, srcst=srcst: e.transpose(pst[:, 0:8], srcst, ident[0:8, 0:8]), reads=[b_acc[0], b_c], writes=[bpst])
                        S.add("dve", lambda e, pst=pst: e.tensor_copy(out=zs[:, :, 0:2], in_=pst[:, 0:8].rearrange("p (b j) -> p b j", j=2)),
                              reads=[bpst], writes=[b_sm])
                        z0, z1, z2 = zs[:, :, 0:4], zs[:, :, 1:5], zs[:, :, 2:6]
                        yv = zb[:, 16:32].rearrange("p (b j) -> p b j", j=4)
                        outc = cT[:, c, 1024:1040].rearrange("p (b j) -> p b j", j=4)
                        pbv = pb_[:, 0:16].rearrange("p (b j) -> p b j", j=4)
                    if not is_s:
                        z0, z1, z2 = zc[:, s0:s0 + sn], zc[:, s0 + 1:s0 + 1 + sn], zc[:, s0 + 2:s0 + 2 + sn]
                        yv = zb[:, 512:512 + sn]
                        outc = cT[:, c, s0:s0 + sn]
                        pbv = pb_[:, 0:sn]
                    S.add("dve", lambda e, yv=yv, z0=z0, w0=w0: e.tensor_scalar(out=yv, in0=z0, scalar1=w0, scalar2=None, op0=ALU.mult),
                          reads=[bz, b_c], writes=[b_pg[11]])
                    S.add("dve", lambda e, yv=yv, z1=z1, w1=w1: e.scalar_tensor_tensor(out=yv, in0=z1, scalar=w1, in1=yv, op0=ALU.mult, op1=ALU.add),
                          reads=[bz, b_pg[11], b_c], writes=[b_pg[11]])
                    S.add("dve", lambda e, yv=yv, z2=z2, w2=w2: e.scalar_tensor_tensor(out=yv, in0=z2, scalar=w2, in1=yv, op0=ALU.mult, op1=ALU.add),
                          reads=[bz, b_pg[11], b_c], writes=[b_pg[11]])
                    S.add("dve", lambda e, yv=yv, outc=outc, pbv=pbv: e.tensor_tensor(out=outc, in0=pbv, in1=yv, op=ALU.mult),
                          reads=[bpb, b_pg[11]], writes=b_cT)
                if pi == 0:
                    S.add("dve", lambda e, c=c: e.tensor_copy(out=sm[:, 64 + 2 * c:66 + 2 * c], in_=zc[:, 1024:1026]), reads=[b_pg[10]], writes=[b_sm])
                if pi == 1:
                    pst, bpst = nps()
                    S.add("dve", lambda e: e.tensor_copy(out=zb[:, 32:40].rearrange("p (b j) -> p b j", j=2), in_=zs[:, :, 4:6]),
                          reads=[b_sm], writes=[b_pg[11]])
                    S.add("pe", lambda e, pst=pst: e.transpose(pst[0:8, 0:128], zb[:, 32:40], ident), reads=[b_pg[11], b_c], writes=[bpst])
                    S.add("dve", lambda e, pst=pst, c=c: e.tensor_copy(out=cvst, in_=pst[0:8, 0:128]), reads=[bpst], writes=[b_acc[1]])
                    S.add("sp", lambda e, c=c: e.dma_start(out=convs[l, :, c * 128:(c + 1) * 128], in_=cvst), reads=[b_acc[1]], dma=True)
            for oc in range(8):
                wa_, bwa = load_w(wao[l], 0, 4, oc * 128, 128, 0)
                slot = (st["w"] - 1) % 3
                wc2, _ = load_w(wco[l], 0, 8, oc * 128, 128, 512, slot=slot)
                wg1, _ = load_w(wl, 0, 8, 7680 + oc * 128, 128, 1536, slot=slot)
                wg2, _ = load_w(wl, 0, 8, 8704 + oc * 128, 128, 2560, slot=slot)
                for (s0, sn) in subs:
                    ti = tis[s0 // 512]
                    t0 = p0 + s0
                    p1, bp1 = nps()
                    for k in range(4):
                        S.add("pe", lambda e, p1=p1, k=k, s0=s0, sn=sn, wa_=wa_: e.matmul(p1[:, 0:sn], lhsT=wa_[:, k, :], rhs=aT[:, k, s0:s0 + sn], start=(k == 0), stop=(k == 3)),
                              reads=[bwa] + b_aT, writes=[bp1])
                    p2, bp2 = nps()
                    for k in range(8):
                        S.add("pe", lambda e, p2=p2, k=k, s0=s0, sn=sn, wc2=wc2: e.matmul(p2[:, 0:sn], lhsT=wc2[:, k, :], rhs=cT[:, k, s0:s0 + sn], start=(k == 0), stop=(k == 7)),
                              reads=[bwa] + b_cT, writes=[bp2])
                    g1p, bg1 = nps()
                    g2p, bg2 = nps()
                    for gp, bgp, wg in ((g1p, bg1, wg1), (g2p, bg2, wg2)):
                        for k in range(8):
                            S.add("pe", lambda e, gp=gp, k=k, t0=t0, sn=sn, wg=wg: e.matmul(gp[:, 0:sn], lhsT=wg[:, k, :], rhs=uv(k, t0, sn), start=(k == 0), stop=(k == 7)),
                                  reads=[bwa, b_u[k][ti]], writes=[bgp])
                    s1, s2 = pagef(10)[:, 0:sn], pagef(11)[:, 0:sn]
                    S.add("act", lambda e, g1p=g1p, s1=s1, sn=sn, oc=oc: e.activation(out=s1, in_=g1p[:, 0:sn], func=AF.Sigmoid,
                                                                                     bias=cf[:, BG + l * 16 + oc:BG + l * 16 + oc + 1]), reads=[bg1, b_c], writes=[b_pg[10]])
                    S.add("act", lambda e, g2p=g2p, s2=s2, sn=sn, oc=oc: e.activation(out=s2, in_=g2p[:, 0:sn], func=AF.Sigmoid,
                                                                                     bias=cf[:, BG + l * 16 + 8 + oc:BG + l * 16 + 8 + oc + 1]), reads=[bg2, b_c], writes=[b_pg[11]])
                    S.add("dve", lambda e, p1=p1, s1=s1, sn=sn: e.tensor_tensor(out=s1, in0=p1[:, 0:sn], in1=s1, op=ALU.mult), reads=[bp1, b_pg[10]], writes=[b_pg[10]])
                    S.add("dve", lambda e, p2=p2, s2=s2, sn=sn: e.tensor_tensor(out=s2, in0=p2[:, 0:sn], in1=s2, op=ALU.mult), reads=[bp2, b_pg[11]], writes=[b_pg[11]])
                    S.add("dve", lambda e, s1=s1, s2=s2, oc=oc, s0=s0, sn=sn: e.tensor_tensor(out=mT[:, oc, s0:s0 + sn], in0=s1, in1=s2, op=ALU.add),
                          reads=[b_pg[10], b_pg[11]], writes=b_mT)
            for oc in range(8):
                wv, bw = load_w(wo[l], 0, 8, oc * 128, 128)
                for (s0, sn) in subs:
                    ti = tis[s0 // 512]
                    t0 = p0 + s0
                    ps, bp = nps()
                    for k in range(8):
                        S.add("pe", lambda e, ps=ps, k=k, s0=s0, sn=sn, wv=wv: e.matmul(ps[:, 0:sn], lhsT=wv[:, k, :], rhs=mT[:, k, s0:s0 + sn], start=(k == 0), stop=(k == 7)),
                              reads=[bw] + b_mT, writes=[bp])
                    S.add("dve", lambda e, ps=ps, oc=oc, t0=t0, sn=sn: e.tensor_tensor(out=xv(oc, t0, sn), in0=ps[:, 0:sn], in1=xv(oc, t0, sn), op=ALU.add),
                          reads=[bp, b_x[oc][ti]], writes=[b_x[oc][ti]])

    for l in range(2):
        if STAGE >= 2 + 10 * l:
            ffn(f1i[l], f1o[l], l * 3 + 0)
        if STAGE >= 3 + 10 * l:
            mixer(l)
        if STAGE >= 9 + 10 * l:
            ffn(f2i[l], f2o[l], l * 3 + 2)
    if STAGE >= 1:
        rmsnorm(6, final=True)
    for blk in range(NBLK):
        r0 = blk * 128
        nr = 128 if blk < 16 else 16
        ti = blk // 4 if blk < 16 else 4
        stg = pagef(2 * (blk % 2), 2)[:, 0:1024]
        bs = [b_pg[2 * (blk % 2)], b_pg[2 * (blk % 2) + 1]]
        for half in range(2):
            ps, bp = nps()
            for cc in range(4):
                c = half * 4 + cc
                S.add("pe", lambda e, ps=ps, c=c, cc=cc, r0=r0, nr=nr: e.transpose(ps[0:nr, cc * 128:(cc + 1) * 128], xv(c, r0, nr), ident),
                      reads=[b_x[c][ti], b_c], writes=[bp])
            S.add("dve" if half else "act",
                  (lambda e, ps=ps, stg=stg, nr=nr, half=half: e.tensor_copy(out=stg[0:nr, half * 512:(half + 1) * 512], in_=ps[0:nr, :])) if half else
                  (lambda e, ps=ps, stg=stg, nr=nr, half=half: e.copy(out=stg[0:nr, half * 512:(half + 1) * 512], in_=ps[0:nr, :])),
                  reads=[bp], writes=bs)
        S.add("sp", lambda e, stg=stg, r0=r0, nr=nr: e.dma_start(out=y[r0:r0 + nr, :], in_=stg[0:nr, :]), reads=bs, dma=True)

    while pending:
        S.add("sp", pending.pop(0), dma=True)
    S.emit(nc, stack)
    stack.close()
    return nc


_NC = None


def _consts():
    cst_f = np.zeros((128, 1280), np.float32)
    cst_f[:, 256:384] = np.eye(128, dtype=np.float32)
    cst_f[:, 384:512] = 1.0 / 1024.0
    for h in range(8):
        cst_f[h, 512 + h * 64:512 + (h + 1) * 64] = 1.0
    cst_f[0:8, 1024:1032] = np.eye(8, dtype=np.float32)
    cst_f[0:8, 1032] = 1.0
    MKS = 1040
    for g in range(3):
        for t in range(4):
            col = MKS + g * 4 + t
            if g == 0:
                cst_f[:, col] = (np.arange(128) >= t).astype(np.float32)
            else:
                cst_f[:, col] = 1.0
    for g in range(3):
        for bt in range(16):
            b, t = bt // 4, bt % 4
            col = MKS + 16 + g * 16 + bt
            for key in range(16):
                kb, kt = key // 4, key % 4
                ok = (kb == b) and ((kt <= t) if g == 0 else (kt == t))
                cst_f[key, col] = 1.0 if ok else 0.0
    cst_s = np.zeros((16, 2048), np.float32)
    for bt in range(16):
        cst_s[bt, bt * 128:(bt + 1) * 128] = 1.0
    k = np.arange(128)[:, None]
    q = np.arange(128)[None, :]
    mprev = (k >= q).astype(np.float32)
    mcur = (k <= q).astype(np.float32)
    return cst_f, cst_s, mprev, mcur


def kernel(x_prompt, x_sample, cache_kv1, cache_kv2, cache_kv3, state_conv,
           ffn1_norm, ffn1_w_in, ffn1_w_out, mix_norm, w_in, b_gate, conv_w,
           w_attn_out, w_conv_out, w_out, ffn2_norm, ffn2_w_in, ffn2_w_out, final_norm):
    global _NC
    if _NC is None:
        _NC = build()
    nc = _NC
    f = lambda a: np.ascontiguousarray(np.asarray(a, dtype=np.float32))
    x_prompt, x_sample = f(x_prompt), f(x_sample)
    caches = [f(cache_kv1), f(cache_kv2), f(cache_kv3)]
    state_conv = f(state_conv)
    cst_f0, cst_s, mprev, mcur = _consts()
    gains = np.stack([f(ffn1_norm)[0], f(mix_norm)[0], f(ffn2_norm)[0], f(ffn1_norm)[1], f(mix_norm)[1], f(ffn2_norm)[1], f(final_norm)])
    cst_f0[:, 0:56] = gains.reshape(7, 8, 128).transpose(2, 0, 1).reshape(128, 56)
    cst_f0[:, 56:88] = f(b_gate).reshape(2, 16, 128).transpose(2, 0, 1).reshape(128, 32)
    cst_f0[:, 88:136] = f(conv_w).reshape(2, 3, 8, 128).transpose(3, 0, 1, 2).reshape(128, 48)
    shared = {"f1i": f(ffn1_w_in), "f1o": f(ffn1_w_out), "f2i": f(ffn2_w_in), "f2o": f(ffn2_w_out), "win": f(w_in),
              "wao": f(w_attn_out), "wco": f(w_conv_out), "wo": f(w_out)}
    in_maps = []
    for c in range(8):
        sq, j = c // 4, c % 4
        xin = np.concatenate([x_prompt[sq, j * 2048:(j + 1) * 2048], x_sample[c * 4:(c + 1) * 4].reshape(16, 1024)], 0)
        cf_ = cst_f0.copy()
        cf_[:, 136] = 1.0 if j == 1 else 0.0
        cf_[:, 137] = 1.0 if j == 2 else 0.0
        cf_[:, 139] = 1.0 if j == 3 else 0.0
        cf_[:, 138] = 0.0 if j == 0 else 1.0
        cbm = np.zeros((128, 768), np.float32)
        NEGM = -30000.0
        cbm[:, 0:128], cbm[:, 128:256] = (1.0 - mprev) * NEGM, (1.0 - mcur) * NEGM
        cbm[:, 256:384] = cbm[:, 0:128] if j != 0 else NEGM
        cbm[:, 384:512] = cbm[:, 128:256]
        cbm[:, 512:640] = 1.0
        cbm[:, 640:768] = np.eye(128, dtype=np.float32)
        m = dict(shared)
        m.update({"xin": np.ascontiguousarray(xin), "cst_f": cf_, "cst_b": cbm.astype(ml_dtypes.bfloat16),
                  "sconv": np.ascontiguousarray(state_conv[:, c * 4:(c + 1) * 4].reshape(2, 8, 1024))})
        for g in range(3):
            m[f"ck{g}"] = np.ascontiguousarray(caches[g][:, c * 4:(c + 1) * 4].reshape(2, 4, LB[g], 1024))
        in_maps.append(m)
    res = run_bass_kernel_spmd(nc, in_maps, core_ids=list(range(8)))
    R = res.results
    yp = np.stack([np.concatenate([R[sq * 4 + j]["y"][0:2048] for j in range(4)], 0) for sq in range(2)])
    ys = np.concatenate([R[c]["y"][2048:2064].reshape(4, 4, 1024) for c in range(8)], 0)
    outs = [yp, ys]
    for g in range(3):
        outs.append(np.stack([R[3][f"kvp{g}"], R[7][f"kvp{g}"]], 1).reshape(2, 2, LB[g], 2, 8, 64))
    outs.append(np.stack([R[3]["convp"], R[7]["convp"]], 1))
    for g in range(3):
        outs.append(np.concatenate([R[c][f"kvs{g}"] for c in range(8)], 1).reshape(2, 32, LB[g], 2, 8, 64))
    outs.append(np.concatenate([R[c]["convs"].reshape(2, 4, 2, 1024) for c in range(8)], 1))
    return tuple(np.ascontiguousarray(o.astype(np.float32)) for o in outs)
```
